# Optimizing a Trainium2 kernel written in Bass

```python
import jax, jax.numpy as jnp
from jax import lax
import numpy as np

D_MODEL = 4096
BATCH = 2
SEQ = 8192
DEPTH = 2

CHUNK = 64
HEAD_DIM = 128
D_MIX = D_MODEL
D_RET = D_MIX // 2
D_ATT = D_MIX - D_RET
N_RET_HEADS = D_RET // HEAD_DIM
N_ATT_HEADS = D_ATT // HEAD_DIM
D_IN = 4 * D_RET + 3 * D_ATT
LEFT_CHUNKS = 8
LEFT = LEFT_CHUNKS * CHUNK
BAND = (LEFT_CHUNKS + 1) * CHUNK
REL_CLIP = 128
N_REL = REL_CLIP + CHUNK
D_FF = 256 * ((8 * D_MODEL // 3 + 255) // 256)
CONV_WIDTH = 3
ROPE_BASE = 10000.0
EPS = 1e-6

kernel_name = "hybrid_retention_chunkattn_convffn"


def rms_norm(x, gain):
    xf = x.astype(jnp.float32)
    y = xf * lax.rsqrt(jnp.mean(xf * xf, axis=-1, keepdims=True) + EPS)
    return (y * gain.astype(jnp.float32)).astype(x.dtype)


def rotary(x):
    t = x.shape[1]
    half = HEAD_DIM // 2
    inv = 1.0 / (ROPE_BASE ** jnp.linspace(0.0, 1.0, half, dtype=jnp.float32))
    ang = jnp.arange(t, dtype=jnp.float32)[:, None] * inv[None, :]
    cos = jnp.cos(ang)[None, :, None, :]
    sin = jnp.sin(ang)[None, :, None, :]
    xf = x.astype(jnp.float32)
    x1, x2 = xf[..., :half], xf[..., half:]
    return jnp.concatenate([x1 * cos - x2 * sin, x1 * sin + x2 * cos], axis=-1)


def retention(q, k, v):
    b, t, h, dh = q.shape
    nc = t // CHUNK
    log_g = jnp.log(1.0 - 2.0 ** (-5.0 - jnp.arange(h, dtype=jnp.float32)))
    pos = jnp.arange(CHUNK, dtype=jnp.float32)
    diff = pos[:, None] - pos[None, :]
    intra = jnp.where(diff[None] >= 0,
                      jnp.exp(jnp.maximum(diff, 0.0)[None] * log_g[:, None, None]), 0.0)
    q_decay = jnp.exp((pos[:, None] + 1.0) * log_g[None, :])
    k_decay = jnp.exp((CHUNK - 1.0 - pos[:, None]) * log_g[None, :])
    chunk_decay = jnp.exp(CHUNK * log_g)

    def to_chunks(a):
        return a.astype(jnp.float32).reshape(b, nc, CHUNK, h, dh).transpose(1, 0, 2, 3, 4)

    qc = to_chunks(q)
    kc = to_chunks(k) * (dh ** -0.5)
    vc = to_chunks(v)

    def step(state, xs):
        qi, ki, vi = xs
        s = jnp.einsum('bihd,bjhd->bhij', qi, ki) * intra[None]
        o = jnp.einsum('bhij,bjhe->bihe', s, vi)
        o = o + jnp.einsum('bihd,bhde->bihe', qi, state) * q_decay[None, :, :, None]
        state = (state * chunk_decay[None, :, None, None]
                 + jnp.einsum('bjhd,bjhe->bhde', ki * k_decay[None, :, :, None], vi))
        return state, o

    s0 = jnp.zeros((b, h, dh, dh), jnp.float32)
    _, o = lax.scan(step, s0, (qc, kc, vc))
    return o.transpose(1, 0, 2, 3, 4).reshape(b, t, h, dh)


def chunk_attention(q, k, v, rel_table):
    b, t, h, dh = q.shape
    nc = t // CHUNK
    qi = jnp.arange(CHUNK)[:, None]
    kj = jnp.arange(BAND)
    rel = jnp.clip(qi + LEFT - kj[None, :], -(CHUNK - 1), REL_CLIP) + (CHUNK - 1)
    bias = rel_table.astype(jnp.float32)[:, rel]
    pad = ((0, 0), (LEFT, 0), (0, 0), (0, 0))
    kp = jnp.pad(k, pad)
    vp = jnp.pad(v, pad)
    qc = q.reshape(b, nc, CHUNK, h, dh).transpose(1, 0, 2, 3, 4)
    scale = dh ** -0.5

    def one_chunk(args):
        c, qb = args
        start = c * CHUNK
        kb = lax.dynamic_slice_in_dim(kp, start, BAND, axis=1)
        vb = lax.dynamic_slice_in_dim(vp, start, BAND, axis=1)
        s = jnp.einsum('bihd,bjhd->bhij', qb, kb).astype(jnp.float32) * scale + bias[None]
        valid = kj >= LEFT - start
        s = jnp.where(valid[None, None, None, :], s, -jnp.inf)
        p = jax.nn.softmax(s, axis=-1).astype(vb.dtype)
        return jnp.einsum('bhij,bjhd->bihd', p, vb)

    o = lax.map(one_chunk, (jnp.arange(nc), qc))
    return o.transpose(1, 0, 2, 3, 4).reshape(b, t, h, dh)


def hybrid_mixer(h, ln, w_in, w_out, rel_table):
    b, t, _ = h.shape
    xn = rms_norm(h, ln)
    proj = xn @ w_in
    splits = [D_RET, 2 * D_RET, 3 * D_RET, 4 * D_RET, 4 * D_RET + D_ATT, 4 * D_RET + 2 * D_ATT]
    rq, rk, rv, rg, aq, ak, av = jnp.split(proj, splits, axis=-1)
    rq = rotary(rq.reshape(b, t, N_RET_HEADS, HEAD_DIM))
    rk = rotary(rk.reshape(b, t, N_RET_HEADS, HEAD_DIM))
    rv = rv.reshape(b, t, N_RET_HEADS, HEAD_DIM)
    ro = retention(rq, rk, rv)
    ro = ro * lax.rsqrt(jnp.mean(ro * ro, axis=-1, keepdims=True) + EPS)
    ro = jax.nn.silu(rg.astype(jnp.float32)).reshape(b, t, N_RET_HEADS, HEAD_DIM) * ro
    ro = ro.reshape(b, t, D_RET).astype(h.dtype)
    ao = chunk_attention(aq.reshape(b, t, N_ATT_HEADS, HEAD_DIM),
                         ak.reshape(b, t, N_ATT_HEADS, HEAD_DIM),
                         av.reshape(b, t, N_ATT_HEADS, HEAD_DIM), rel_table)
    ao = ao.reshape(b, t, D_ATT).astype(h.dtype)
    return jnp.concatenate([ro, ao], axis=-1) @ w_out


def conv_ffn(h, ln, w_up, conv_w, conv_b, w_down):
    xn = rms_norm(h, ln)
    u = xn @ w_up
    u = lax.conv_general_dilated(u, conv_w[:, None, :], window_strides=(1,),
                                 padding=[(CONV_WIDTH - 1, 0)],
                                 dimension_numbers=('NWC', 'WIO', 'NWC'),
                                 feature_group_count=2 * D_FF) + conv_b
    g, val = jnp.split(u, 2, axis=-1)
    return (jax.nn.silu(g) * val) @ w_down


def setup_inputs(seed: int = 0) -> dict:
    key = jax.random.key(seed)
    ks = jax.random.split(key, 12)
    res_scale = (2.0 * DEPTH) ** -0.5
    f32 = jnp.float32
    x = jax.random.normal(ks[0], (BATCH, SEQ, D_MODEL), f32)
    ln_mix = 1.0 + 0.02 * jax.random.normal(ks[1], (DEPTH, D_MODEL), f32)
    w_in = jax.random.normal(ks[2], (DEPTH, D_MODEL, D_IN), f32) * D_MODEL ** -0.5
    rel_bias = 0.2 * jax.random.normal(ks[3], (DEPTH, N_ATT_HEADS, N_REL), f32)
    w_out = jax.random.normal(ks[4], (DEPTH, D_MIX, D_MODEL), f32) * (D_MIX ** -0.5 * res_scale)
    ln_ffn = 1.0 + 0.02 * jax.random.normal(ks[5], (DEPTH, D_MODEL), f32)
    w_up = jax.random.normal(ks[6], (DEPTH, D_MODEL, 2 * D_FF), f32) * D_MODEL ** -0.5
    conv_w = jax.random.normal(ks[7], (DEPTH, CONV_WIDTH, 2 * D_FF), f32) * CONV_WIDTH ** -0.5
    conv_b = 0.02 * jax.random.normal(ks[8], (DEPTH, 2 * D_FF), f32)
    w_down = jax.random.normal(ks[9], (DEPTH, D_FF, D_MODEL), f32) * (D_FF ** -0.5 * res_scale)
    ln_final = 1.0 + 0.02 * jax.random.normal(ks[10], (D_MODEL,), f32)
    return {"x": x, "ln_mix": ln_mix, "w_in": w_in, "rel_bias": rel_bias, "w_out": w_out,
            "ln_ffn": ln_ffn, "w_up": w_up, "conv_w": conv_w, "conv_b": conv_b,
            "w_down": w_down, "ln_final": ln_final}


def reference(x, ln_mix, w_in, rel_bias, w_out, ln_ffn, w_up, conv_w, conv_b, w_down, ln_final):
    h = x
    for layer in range(DEPTH):
        h = h + hybrid_mixer(h, ln_mix[layer], w_in[layer], w_out[layer], rel_bias[layer])
        h = h + conv_ffn(h, ln_ffn[layer], w_up[layer], conv_w[layer], conv_b[layer], w_down[layer])
    return rms_norm(h, ln_final)
```

```python
import contextlib
import numpy as np
import ml_dtypes
import concourse.bass as bass
import concourse.mybir as mybir
from concourse.bass_utils import run_bass_kernel_spmd

F32 = mybir.dt.float32
BF16 = mybir.dt.bfloat16
U8 = mybir.dt.uint8
AF = mybir.ActivationFunctionType
ALU = mybir.AluOpType

NCORES = 8
D = 4096
NCC = 32
DFF = 11008
NFC = 86
SEQ = 8192
BATCH = 2
TOK = BATCH * SEQ
TOKC = TOK // NCORES
HALO = 32
TF = TOKC + HALO
PASS = 512
TP = PASS + HALO
NPASS = TOKC // PASS
EPS = 1e-6
HD = 128
MT = 256
NEG = -30000.0


class Buf:
    __slots__ = ("name", "w", "r", "dsem", "dcnt", "const")

    def __init__(self, name, const=False):
        self.name = name
        self.w = None
        self.r = []
        self.dsem = None
        self.dcnt = 0
        self.const = const


class Prog:
    ENG = ("pe", "act", "dve", "pool", "sp")
    CENG = ("pe", "act", "dve", "pool")

    def __init__(self, nc, stack):
        self.nc = nc
        self.stack = stack
        self.q = {e: [] for e in self.ENG}
        self.seen = {e: {} for e in self.ENG}
        self.sem = {}
        self.cnt = {}
        for e in self.CENG:
            self.sem[e] = stack.enter_context(nc.semaphore("pg_" + e))
            self.cnt[e] = 0
        self.dma_bufs = []
        self.nsem = 0

    def buf(self, name, const=False):
        return Buf(name, const)

    def _dsem(self, b):
        if b.dsem is None:
            b.dsem = self.stack.enter_context(self.nc.semaphore("d%d" % self.nsem))
            self.nsem += 1
            self.dma_bufs.append(b)
        return b.dsem

    def _collect(self, eng, reads, writes, extra=()):
        w = list(extra)
        for b in reads:
            if b.w is not None:
                w.append(b.w)
        for b in writes:
            w.extend(b.r)
            if b.w is not None:
                w.append(b.w)
        best = {}
        for (s, v) in w:
            k = id(s)
            if k not in best or best[k][1] < v:
                best[k] = (s, v)
        out = []
        seen = self.seen[eng]
        for k, (s, v) in best.items():
            if eng == "pe" and s is self.sem["pe"]:
                continue
            if seen.get(k, -1) >= v:
                continue
            seen[k] = v
            out.append((s, v))
        return out

    @staticmethod
    def _addr(b, tok):
        if b.const:
            return
        for i, (s, v) in enumerate(b.r):
            if s is tok[0]:
                b.r[i] = tok
                return
        b.r.append(tok)

    def _record(self, tok, reads, writes):
        for b in reads:
            self._addr(b, tok)
        for b in writes:
            b.w = tok
            b.r = []

    def op(self, eng, fn, reads=(), writes=()):
        waits = self._collect(eng, reads, writes)
        self.cnt[eng] += 1
        tok = (self.sem[eng], self.cnt[eng])
        self.q[eng].append((waits, fn, tok, 1))
        self._record(tok, reads, writes)
        return tok

    def pe_group(self, fns, reads=(), writes=()):
        waits = self._collect("pe", reads, writes)
        self.cnt["pe"] += 1
        tok = (self.sem["pe"], self.cnt["pe"])
        n = len(fns)
        for i, fn in enumerate(fns):
            self.q["pe"].append((waits if i == 0 else [], fn, tok if i == n - 1 else None, 1))
        self._record(tok, reads, writes)
        return tok

    def dma(self, queue, out, in_, own, reads=(), writes=(), **kw):
        waits = self._collect(queue, reads, writes)
        sem = self._dsem(own)
        own.dcnt += 16
        tok = (sem, own.dcnt)
        self.q[queue].append((waits, (lambda e, o=out, i=in_, k=kw: e.dma_start(out=o, in_=i, **k)), tok, 16))
        self._record(tok, reads, writes)
        return tok

    def barrier(self):
        toks = [(self.sem[e], self.cnt[e]) for e in self.CENG if self.cnt[e] > 0]
        toks += [(b.dsem, b.dcnt) for b in self.dma_bufs if b.dcnt > 0]
        for e in self.ENG:
            waits = []
            seen = self.seen[e]
            for (s, v) in toks:
                if e == "pe" and s is self.sem["pe"]:
                    continue
                if seen.get(id(s), -1) >= v:
                    continue
                seen[id(s)] = v
                waits.append((s, v))
            if waits:
                self.q[e].append((waits, None, None, 0))

    def emit(self, block):
        def run(eng, items):
            for waits, fn, tok, inc in items:
                for (s, v) in waits:
                    eng.wait_ge(s, v)
                if fn is not None:
                    ins = fn(eng)
                    if tok is not None:
                        ins.then_inc(tok[0], inc)

        q = self.q

        @block.tensor
        def _(e):
            run(e, q["pe"])

        @block.scalar
        def _(e):
            run(e, q["act"])

        @block.vector
        def _(e):
            run(e, q["dve"])

        @block.gpsimd
        def _(e):
            run(e, q["pool"])

        @block.sync
        def _(e):
            run(e, q["sp"])


class Arena:
    def __init__(self, tensor, nbytes):
        self.t = tensor
        self.n = nbytes
        self.off = 0

    def at(self, off, nbytes, dt, pat=None, **kw):
        assert off + nbytes <= self.n, (off, nbytes, self.n)
        ap = self.t[:, off:off + nbytes]
        if dt is not U8:
            ap = ap.bitcast(dt)
        if pat:
            ap = ap.rearrange(pat, **kw)
        return ap

    def take(self, nbytes, dt, pat=None, **kw):
        nb = (nbytes + 31) // 32 * 32
        ap = self.at(self.off, nbytes, dt, pat, **kw)
        self.off += nb
        return ap


def _mm(out, lhsT, rhs, start, stop):
    return lambda e: e.matmul(out, lhsT, rhs, start=start, stop=stop)


def _tr(out, in_, ident):
    return lambda e: e.transpose(out, in_, ident)


def emit_rstd(P, src, srcbuf, n, width, junk, junkbuf, st, stbuf):
    P.op("act", lambda e: e.activation(out=junk[:n], in_=src[:n], func=AF.Square, accum_out=st[:n, 0:1]),
         reads=[srcbuf], writes=[junkbuf, stbuf])
    P.op("dve", lambda e: e.tensor_scalar(out=st[:n, 1:2], in0=st[:n, 0:1], scalar1=1.0 / width, scalar2=EPS,
                                          op0=ALU.mult, op1=ALU.add), reads=[stbuf], writes=[stbuf])
    P.op("act", lambda e: e.activation(out=st[:n, 2:3], in_=st[:n, 1:2], func=AF.Sqrt), reads=[stbuf], writes=[stbuf])
    P.op("dve", lambda e: e.reciprocal(out=st[:n, 3:4], in_=st[:n, 2:3]), reads=[stbuf], writes=[stbuf])


def emit_norm_T(P, hrow, hbuf, n, st, stbuf, junk, junkbuf, gfm, identf, ps, psbufs, dst, dstbuf, col0, pscur):
    emit_rstd(P, hrow, hbuf, n, D, junk, junkbuf, st, stbuf)
    P.op("dve", lambda e: e.tensor_scalar(out=hrow[:n], in0=hrow[:n], scalar1=st[:n, 3:4], scalar2=None, op0=ALU.mult),
         reads=[hbuf, stbuf], writes=[hbuf])
    for g4 in range(NCC // 4):
        bi, bb = psbufs[pscur[0] % len(psbufs)]
        pscur[0] += 1
        bank = ps[:, bi * 512:(bi + 1) * 512]
        fns = []
        for k in range(4):
            cc = g4 * 4 + k
            fns.append(_tr(bank[:, k * 128:k * 128 + n], hrow[:n, cc * 128:(cc + 1) * 128], identf[:n, :n]))
        P.pe_group(fns, reads=[hbuf], writes=[bb])
        for k in range(4):
            cc = g4 * 4 + k
            o = dst[:, cc, col0:col0 + n]
            i = bank[:, k * 128:k * 128 + n]
            if k % 2 == 0:
                P.op("act", lambda e, o=o, i=i, cc=cc: e.activation(out=o, in_=i, func=AF.Copy, scale=gfm[:, cc:cc + 1]),
                     reads=[bb], writes=[dstbuf])
            else:
                P.op("dve", lambda e, o=o, i=i, cc=cc: e.tensor_scalar(out=o, in0=i, scalar1=gfm[:, cc:cc + 1], scalar2=None,
                                                                   op0=ALU.mult), reads=[bb], writes=[dstbuf])


def build_N():
    nc = bass.Bass("TRN2", target_bir_lowering=False)
    rows = nc.dram_tensor("rows", [TOKC, D], F32, kind="ExternalInput").ap()
    gfm_d = nc.dram_tensor("gfm", [128, NCC], F32, kind="ExternalInput").ap()
    idf_d = nc.dram_tensor("identf", [128, 128], F32, kind="ExternalInput").ap()
    xT_d = nc.dram_tensor("xT", [D, TOKC], BF16, kind="ExternalOutput").ap()
    xT_r = xT_d.rearrange("(cc p) t -> p cc t", p=128)
    SB = 150 * 1024
    with contextlib.ExitStack() as stack:
        arena_t = stack.enter_context(nc.sbuf_tensor("arena", [128, SB], U8))
        ps_t = stack.enter_context(nc.psum_tensor("ps", [128, 4096], F32))
        ps = ps_t[:, :]
        A = Arena(arena_t, SB)
        P = Prog(nc, stack)
        gfm = A.take(NCC * 4, F32)
        identf = A.take(128 * 4, F32)
        st = [A.take(16, F32) for _ in range(2)]
        junk = A.take(D * 2, BF16)
        hrow = [A.take(D * 4, F32) for _ in range(2)]
        xst = [A.take(NCC * 512 * 2, BF16, "p (c t) -> p c t", c=NCC) for _ in range(2)]
        b_c = P.buf("consts", const=True)
        b_st = [P.buf("st%d" % i) for i in range(2)]
        b_junk = P.buf("junk")
        b_h = [P.buf("h%d" % i) for i in range(2)]
        b_x = [P.buf("x%d" % i) for i in range(2)]
        psb = [(i, P.buf("ps%d" % i)) for i in range(8)]
        pscur = [0]
        P.dma("sp", gfm, gfm_d, own=b_c, writes=[b_c])
        b_c2 = P.buf("consts2", const=True)
        P.dma("sp", identf, idf_d, own=b_c2, writes=[b_c2])
        b_c.const = True
        ntile = TOKC // 128
        for i in range(ntile):
            hb = i % 2
            P.dma("sp", hrow[hb], rows[i * 128:(i + 1) * 128, :], own=b_h[hb], writes=[b_h[hb]])
            xs = (i // 4) % 2
            if i == 0:
                for e in ("act", "dve", "pe"):
                    P.seen[e]
            emit_norm_T(P, hrow[hb], b_h[hb], 128, st[hb], b_st[hb], junk, b_junk, gfm, identf, ps, psb,
                        xst[xs], b_x[xs], (i % 4) * 128, pscur)
            if i % 4 == 3:
                t0 = (i // 4) * 512
                P.dma("sp", xT_r[:, :, t0:t0 + 512], xst[xs], own=b_x[xs], reads=[b_x[xs]])
        P.barrier()
        block = stack.enter_context(nc.Block())
        for e in ("pe", "act", "dve"):
            P.q[e].insert(0, ([(b_c.dsem, 16), (b_c2.dsem, 16)], None, None, 0))
        P.emit(block)
    return nc


def build_M():
    nc = bass.Bass("TRN2", target_bir_lowering=False)
    xnT_d = nc.dram_tensor("xnT", [D, TOK], BF16, kind="ExternalInput").ap()
    win_d = nc.dram_tensor("win", [128, NCC * 1792], F32, kind="ExternalInput").ap()
    cs_d = nc.dram_tensor("cs", [SEQ, 128], F32, kind="ExternalInput").ap()
    dec_d = nc.dram_tensor("dec", [128, 8], F32, kind="ExternalInput").ap()
    rmask_d = nc.dram_tensor("rmask", [128, 256], F32, kind="ExternalInput").ap()
    biasg_d = nc.dram_tensor("biasg", [128, 2 * 640], F32, kind="ExternalInput").ap()
    amask_d = nc.dram_tensor("amask", [128, 640], F32, kind="ExternalInput").ap()
    idb_d = nc.dram_tensor("identb", [128, 128], BF16, kind="ExternalInput").ap()
    mixT_d = nc.dram_tensor("mixT", [512, TOK], BF16, kind="ExternalOutput").ap()
    xnT_r = xnT_d.rearrange("(cc p) t -> p cc t", p=128)
    mixT_r = mixT_d.rearrange("(k p) t -> p k t", p=128)
    SB = 200 * 1024
    scale = float(HD) ** -0.5
    with contextlib.ExitStack() as stack:
        arena_t = stack.enter_context(nc.sbuf_tensor("arena", [128, SB], U8))
        ps_t = stack.enter_context(nc.psum_tensor("ps", [128, 4096], F32))
        ps = ps_t[:, :]
        A = Arena(arena_t, SB)
        P = Prog(nc, stack)
        win = A.take(NCC * 1792 * 2, BF16, "p (c f) -> p c f", c=NCC)
        xt = [A.take(NCC * MT * 2, BF16, "p (c t) -> p c t", c=NCC) for _ in range(2)]
        dec = A.take(8 * 4, F32)
        rmask = A.take(256 * 4, F32, "p (h i) -> p h i", h=2)
        biasf = A.take(2 * 640 * 4, F32, "p (h k) -> p h k", h=2)
        amask = A.take(640 * 4, F32)
        identb = A.take(128 * 2, BF16)
        cst = [A.take(128 * 4, F32) for _ in range(2)]
        rot = A.take(512 * 4, F32, "p (j h d) -> p j h d", j=4, h=2)
        tmpa = A.take(256 * 4, F32, "p (j d) -> p j d", j=4)
        tmpb = A.take(256 * 4, F32, "p (j d) -> p j d", j=4)
        qkb2 = [A.take(512 * 2, BF16, "p (j d) -> p j d", j=4) for _ in range(2)]
        qkT2 = [A.take(512 * 2, BF16, "p (j d) -> p j d", j=4) for _ in range(2)]
        vb2 = [A.take(256 * 2, BF16, "p (h d) -> p h d", h=2) for _ in range(2)]
        sg2 = [A.take(256 * 4, F32, "p (h d) -> p h d", h=2) for _ in range(2)]
        stm = A.take(256 * 2, BF16, "p (h d) -> p h d", h=2)
        state = A.take(256 * 4, F32, "p (h d) -> p h d", h=2)
        stateb = A.take(256 * 2, BF16, "p (h d) -> p h d", h=2)
        junk = A.take(128 * 4, F32)
        rst = [A.take(16, F32) for _ in range(2)]
        rob = A.take(256 * 2, BF16, "p (h d) -> p h d", h=2)
        aob = A.take(256 * 2, BF16, "p (h d) -> p h d", h=2)
        NSLOT = 8
        akT = A.take(2 * NSLOT * 128 * 2, BF16, "p (h s d) -> p h s d", h=2, s=NSLOT)
        avr = A.take(2 * NSLOT * 132 * 2, BF16, "p (h s d) -> p h s d", h=2, s=NSLOT)
        aqT = [A.take(2 * MT * 2, BF16, "p (h t) -> p h t", h=2) for _ in range(2)]
        ssb = [A.take(640 * 4, F32) for _ in range(2)]
        ptb = [A.take(640 * 2, BF16) for _ in range(2)]
        rinv = [A.take(16, F32) for _ in range(2)]
        mst = [A.take(4 * MT * 2, BF16, "p (k t) -> p k t", k=4) for _ in range(2)]

        bc = {}
        for nm in ("win", "dec", "rmask", "biasg", "amask", "identb"):
            bc[nm] = P.buf(nm, const=True)
        b_xt = [P.buf("xt%d" % i) for i in range(2)]
        b_cs = [P.buf("cs%d" % i) for i in range(2)]
        b_rot, b_ta, b_tb = (P.buf(n) for n in ("rot", "ta", "tb"))
        b_qkb2 = [P.buf("qkb%d" % i) for i in range(2)]
        b_qkT2 = [P.buf("qkT%d" % i) for i in range(2)]
        b_vb2 = [P.buf("vb%d" % i) for i in range(2)]
        b_sg2 = [P.buf("sg%d" % i) for i in range(2)]
        b_stm = [P.buf("stm%d" % h) for h in range(2)]
        b_state = [P.buf("state%d" % h) for h in range(2)]
        b_stateb = [P.buf("stateb%d" % h) for h in range(2)]
        b_junk = P.buf("junk")
        b_rst = [P.buf("rst%d" % h) for h in range(2)]
        b_rob = [P.buf("rob%d" % h) for h in range(2)]
        b_aob = [P.buf("aob%d" % h) for h in range(2)]
        b_akT = [[P.buf("akT%d_%d" % (h, s)) for s in range(NSLOT)] for h in range(2)]
        b_avr = [[P.buf("avr%d_%d" % (h, s)) for s in range(NSLOT)] for h in range(2)]
        b_aqT = [P.buf("aqT%d" % i) for i in range(2)]
        b_ssb = [P.buf("ssb%d" % i) for i in range(2)]
        b_ptb = [P.buf("ptb%d" % i) for i in range(2)]
        b_rinv = [P.buf("rinv%d" % i) for i in range(2)]
        b_mst = [P.buf("mst%d" % i) for i in range(2)]
        bank = lambda i: ps[:, i * 512:(i + 1) * 512]
        psA, psB, psC = bank(0), bank(1), bank(2)
        psF = [bank(3), bank(4)]
        psT = bank(5).bitcast(BF16)
        psR = bank(6)
        psS = bank(7)
        b_psA, b_psB, b_psC = P.buf("psA"), P.buf("psB"), P.buf("psC")
        b_psF = [P.buf("psF0"), P.buf("psF1")]
        b_psT1, b_psT2 = P.buf("psT1"), P.buf("psT2")
        b_psRs = [P.buf("psRs%d" % h) for h in range(2)]
        b_psRo = [P.buf("psRo%d" % h) for h in range(2)]
        b_psSu = [P.buf("psSu%d" % h) for h in range(2)]
        b_psPV = [P.buf("psPV%d" % h) for h in range(2)]
        psPV = [psC[:, 256:256 + 129], psS[:, 256:256 + 129]]

        for c4 in range(8):
            P.dma("pool", win[:, c4 * 4:(c4 + 1) * 4, :],
                  win_d[:, c4 * 4 * 1792:(c4 + 1) * 4 * 1792].rearrange("p (c f) -> p c f", c=4),
                  own=bc["win"], writes=[bc["win"]], max_dma_last_dim=4096)
        P.dma("sp", dec, dec_d, own=bc["dec"], writes=[bc["dec"]])
        P.dma("sp", rmask, rmask_d.rearrange("p (h i) -> p h i", h=2), own=bc["rmask"], writes=[bc["rmask"]])
        P.dma("sp", biasf, biasg_d.rearrange("p (h k) -> p h k", h=2), own=bc["biasg"], writes=[bc["biasg"]])
        P.dma("sp", amask, amask_d, own=bc["amask"], writes=[bc["amask"]])
        P.dma("sp", identb, idb_d, own=bc["identb"], writes=[bc["identb"]])
        b_bias = P.buf("bias")
        for h in range(2):
            P.op("dve", lambda e, h=h: e.tensor_tensor(out=biasf[:, h, :], in0=biasf[:, h, :], in1=amask, op=ALU.add),
                 reads=[bc["biasg"], bc["amask"]], writes=[b_bias])
        b_bias.const = True
        b_ones = P.buf("ones")
        P.op("dve", lambda e: e.memset(avr[:, :, :, 128:129], 1.0), writes=[b_ones])
        b_ones.const = True

        ntile = TOK // MT
        tiles_per_seq = SEQ // MT
        P.dma("sp", xt[0], xnT_r[:, :, 0:MT], own=b_xt[0], writes=[b_xt[0]])
        for T in range(ntile):
            g0 = T * MT
            tb = T % 2
            Ts = T % tiles_per_seq
            if T + 1 < ntile:
                P.dma("sp", xt[1 - tb], xnT_r[:, :, g0 + MT:g0 + 2 * MT], own=b_xt[1 - tb], writes=[b_xt[1 - tb]])
            if Ts == 0:
                for h in range(2):
                    P.op("dve", lambda e, h=h: e.memset(state[:, h, :], 0.0), writes=[b_state[h]])
                    P.op("act", lambda e, h=h: e.copy(out=stateb[:, h, :], in_=state[:, h, :]),
                         reads=[b_state[h]], writes=[b_stateb[h]])
            for k in range(4):
                pf = psF[k // 2]
                o = pf[:, (k % 2) * MT:(k % 2 + 1) * MT]
                fns = [_mm(o, win[:, cc, 1024 + k * 128:1024 + (k + 1) * 128], xt[tb][:, cc, :], cc == 0, cc == NCC - 1)
                       for cc in range(NCC)]
                P.pe_group(fns, reads=[b_xt[tb], bc["win"]], writes=[b_psF[k // 2]])
            for h in range(2):
                P.op("act", lambda e, h=h, tb=tb: e.activation(out=aqT[tb][:, h, :], in_=psF[0][:, h * MT:(h + 1) * MT],
                                                        func=AF.Copy, scale=scale), reads=[b_psF[0]], writes=[b_aqT[tb]])
            for s in range(2):
                m = 2 * Ts + s
                slot = m % NSLOT
                for h in range(2):
                    P.op("dve", lambda e, h=h, s=s, slot=slot: e.tensor_copy(
                        out=akT[:, h, slot, :], in_=psF[1][:, h * MT + s * 128:h * MT + (s + 1) * 128]),
                        reads=[b_psF[1]], writes=[b_akT[h][slot]])
            def sub(s, phase, T=T, tb=tb, Ts=Ts, g0=g0):
                qkb, qkT, vb, sg = qkb2[s], qkT2[s], vb2[s], sg2[s]
                b_qkb, b_qkT, b_vb, b_sg = b_qkb2[s], b_qkT2[s], b_vb2[s], b_sg2[s]
                m = 2 * Ts + s
                slot = m % NSLOT
                pos = Ts * MT + s * 128
                cb_ = (2 * T + s) % 2
                cs_t = cst[cb_]
                if phase == 1:
                    P.dma("sp", cs_t, cs_d[pos:pos + 128, :], own=b_cs[cb_], writes=[b_cs[cb_]])
                    lhs = lambda cc: xt[tb][:, cc, s * 128:(s + 1) * 128]
                    P.pe_group([_mm(psA, lhs(cc), win[:, cc, 0:512], cc == 0, cc == NCC - 1) for cc in range(NCC)],
                               reads=[b_xt[tb], bc["win"]], writes=[b_psA])
                    P.pe_group([_mm(psB, lhs(cc), win[:, cc, 512:1024], cc == 0, cc == NCC - 1) for cc in range(NCC)],
                               reads=[b_xt[tb]], writes=[b_psB])
                    P.pe_group([_mm(psC[:, 0:256], lhs(cc), win[:, cc, 1536:1792], cc == 0, cc == NCC - 1) for cc in range(NCC)],
                               reads=[b_xt[tb]], writes=[b_psC])
                    pa = psA.rearrange("p (j h d) -> p j h d", j=4, h=2)
                    cosb = cs_t[:, 0:64].unsqueeze(1).broadcast_to([128, 4, 64])
                    sinb = cs_t[:, 64:128].unsqueeze(1).broadcast_to([128, 4, 64])
                    x1, x2 = pa[:, :, 0, :], pa[:, :, 1, :]
                    P.op("dve", lambda e, x1=x1, cosb=cosb: e.tensor_tensor(out=tmpa, in0=x1, in1=cosb, op=ALU.mult),
                         reads=[b_psA, b_cs[cb_]], writes=[b_ta])
                    P.op("dve", lambda e, x2=x2, sinb=sinb: e.tensor_tensor(out=tmpb, in0=x2, in1=sinb, op=ALU.mult),
                         reads=[b_psA, b_cs[cb_]], writes=[b_tb])
                    P.op("dve", lambda e: e.tensor_tensor(out=rot[:, :, 0, :], in0=tmpa, in1=tmpb, op=ALU.subtract),
                         reads=[b_ta, b_tb], writes=[b_rot])
                    P.op("dve", lambda e, x1=x1, sinb=sinb: e.tensor_tensor(out=tmpa, in0=x1, in1=sinb, op=ALU.mult),
                         reads=[b_psA, b_cs[cb_]], writes=[b_ta])
                    P.op("dve", lambda e, x2=x2, cosb=cosb: e.tensor_tensor(out=tmpb, in0=x2, in1=cosb, op=ALU.mult),
                         reads=[b_psA, b_cs[cb_]], writes=[b_tb])
                    P.op("dve", lambda e: e.tensor_tensor(out=rot[:, :, 1, :], in0=tmpa, in1=tmpb, op=ALU.add),
                         reads=[b_ta, b_tb], writes=[b_rot])
                    for j in range(4):
                        P.op("act", lambda e, j=j: e.activation(out=qkb[:, j, :], in_=rot[:, j, :, :].rearrange("p h d -> p (h d)"),
                                                                func=AF.Copy, scale=dec[:, j:j + 1]),
                             reads=[b_rot, bc["dec"]], writes=[b_qkb])
                    P.pe_group([_tr(psT[:, j * 128:(j + 1) * 128], qkb[:, j, :], identb) for j in range(4)],
                               reads=[b_qkb, bc["identb"]], writes=[b_psT1])
                    P.op("dve", lambda e: e.tensor_copy(out=qkT.rearrange("p j d -> p (j d)"), in_=psT[:, 0:512]),
                         reads=[b_psT1], writes=[b_qkT])
                    P.op("act", lambda e: e.copy(out=vb.rearrange("p h d -> p (h d)"), in_=psB[:, 0:256]), reads=[b_psB], writes=[b_vb])
                    P.op("act", lambda e: e.activation(out=sg.rearrange("p h d -> p (h d)"), in_=psB[:, 256:512], func=AF.Silu),
                         reads=[b_psB], writes=[b_sg])
                    for h in range(2):
                        P.op("act", lambda e, h=h, slot=slot: e.copy(out=avr[:, h, slot, 0:128], in_=psC[:, h * 128:(h + 1) * 128]),
                             reads=[b_psC], writes=[b_avr[h][slot]])
                else:
                    mb = (2 * T + s) % 2
                    for h in range(2):
                        P.pe_group([_mm(psR[:, h * 128:(h + 1) * 128], qkT[:, 2 + h, :], qkT[:, h, :], True, True)],
                                   reads=[b_qkT], writes=[b_psRs[h]])
                        P.op("dve", lambda e, h=h: e.tensor_tensor(out=stm[:, h, :], in0=psR[:, h * 128:(h + 1) * 128],
                                                                   in1=rmask[:, h, :], op=ALU.mult),
                             reads=[b_psRs[h], bc["rmask"]], writes=[b_stm[h]])
                        po = psR[:, 256 + h * 128:256 + (h + 1) * 128]
                        P.pe_group([_mm(po, stm[:, h, :], vb[:, h, :], True, False),
                                    _mm(po, qkT[:, h, :], stateb[:, h, :], False, True)],
                                   reads=[b_stm[h], b_vb, b_qkT, b_stateb[h]], writes=[b_psRo[h]])
                        pu = psS[:, h * 128:(h + 1) * 128]
                        P.pe_group([_mm(pu, qkb[:, 2 + h, :], vb[:, h, :], True, True)], reads=[b_qkb, b_vb], writes=[b_psSu[h]])
                        P.op("dve", lambda e, h=h, pu=pu: e.scalar_tensor_tensor(out=state[:, h, :], in0=state[:, h, :],
                                                                               scalar=dec[:, 4 + h:5 + h], in1=pu,
                                                                               op0=ALU.mult, op1=ALU.add),
                             reads=[b_state[h], b_psSu[h], bc["dec"]], writes=[b_state[h]])
                        P.op("act", lambda e, h=h: e.copy(out=stateb[:, h, :], in_=state[:, h, :]),
                             reads=[b_state[h]], writes=[b_stateb[h]])
                        r_ = rst[h]
                        P.op("act", lambda e, po=po, r_=r_: e.activation(out=junk, in_=po, func=AF.Square, accum_out=r_[:, 0:1]),
                             reads=[b_psRo[h]], writes=[b_junk, b_rst[h]])
                        P.op("dve", lambda e, r_=r_: e.tensor_scalar(out=r_[:, 1:2], in0=r_[:, 0:1], scalar1=1.0 / HD, scalar2=EPS,
                                                                    op0=ALU.mult, op1=ALU.add), reads=[b_rst[h]], writes=[b_rst[h]])
                        P.op("act", lambda e, r_=r_: e.activation(out=r_[:, 2:3], in_=r_[:, 1:2], func=AF.Sqrt),
                             reads=[b_rst[h]], writes=[b_rst[h]])
                        P.op("dve", lambda e, r_=r_: e.reciprocal(out=r_[:, 3:4], in_=r_[:, 2:3]), reads=[b_rst[h]], writes=[b_rst[h]])
                        P.op("dve", lambda e, h=h, po=po, r_=r_: e.scalar_tensor_tensor(out=rob[:, h, :], in0=po, scalar=r_[:, 3:4],
                                                                                      in1=sg[:, h, :], op0=ALU.mult, op1=ALU.mult),
                             reads=[b_psRo[h], b_rst[h], b_sg], writes=[b_rob[h]])
                    for h in range(2):
                        ab = (2 * (2 * T + s) + h) % 2
                        r0 = max(0, 4 - m)
                        fx = psF[0]
                        fy = psF[1]
                        for r in range(r0, 5):
                            kslot = (m - 4 + r) % NSLOT
                            o = fx[:, r * 128:(r + 1) * 128] if r < 4 else fy[:, 0:128]
                            P.pe_group([_mm(o, akT[:, h, kslot, :], aqT[tb][:, h, s * 128:(s + 1) * 128], True, True)],
                                       reads=[b_akT[h][kslot], b_aqT[tb]], writes=[b_psF[0] if r < 4 else b_psF[1]])
                        if r0 < 4:
                            P.op("dve", lambda e, h=h, ab=ab, r0=r0: e.tensor_tensor(
                                out=ssb[ab][:, r0 * 128:512], in0=psF[0][:, r0 * 128:512], in1=biasf[:, h, r0 * 128:512], op=ALU.add),
                                reads=[b_psF[0], b_bias], writes=[b_ssb[ab]])
                        P.op("dve", lambda e, h=h, ab=ab: e.tensor_tensor(
                            out=ssb[ab][:, 512:640], in0=psF[1][:, 0:128], in1=biasf[:, h, 512:640], op=ALU.add),
                            reads=[b_psF[1], b_bias], writes=[b_ssb[ab]])
                        P.op("act", lambda e, ab=ab, r0=r0: e.activation(out=ptb[ab][:, r0 * 128:640], in_=ssb[ab][:, r0 * 128:640],
                                                                        func=AF.Exp), reads=[b_ssb[ab]], writes=[b_ptb[ab]])
                        fns = []
                        rd = [b_ptb[ab], b_ones]
                        for r in range(r0, 5):
                            kslot = (m - 4 + r) % NSLOT
                            fns.append(_mm(psPV[h], ptb[ab][:, r * 128:(r + 1) * 128], avr[:, h, kslot, 0:129], r == r0, r == 4))
                            rd.append(b_avr[h][kslot])
                        P.pe_group(fns, reads=rd, writes=[b_psPV[h]])
                        P.op("dve", lambda e, h=h, ab=ab: e.reciprocal(out=rinv[ab][:, 0:1], in_=psPV[h][:, 128:129]),
                             reads=[b_psPV[h]], writes=[b_rinv[ab]])
                        P.op("dve", lambda e, h=h, ab=ab: e.tensor_scalar(out=aob[:, h, :], in0=psPV[h][:, 0:128],
                                                                         scalar1=rinv[ab][:, 0:1], scalar2=None, op0=ALU.mult),
                             reads=[b_psPV[h], b_rinv[ab]], writes=[b_aob[h]])
                    fns = [_tr(psT[:, (4 + h) * 128:(5 + h) * 128], rob[:, h, :], identb) for h in range(2)]
                    fns += [_tr(psT[:, (6 + h) * 128:(7 + h) * 128], aob[:, h, :], identb) for h in range(2)]
                    P.pe_group(fns, reads=[b_rob[0], b_rob[1], b_aob[0], b_aob[1]], writes=[b_psT2])
                    P.op("act", lambda e, tb=tb, s=s: e.copy(out=mst[tb][:, :, s * 128:(s + 1) * 128],
                                                            in_=psT[:, 512:1024].rearrange("p (k t) -> p k t", k=4)),
                         reads=[b_psT2], writes=[b_mst[tb]])

            sub(0, 1)
            sub(1, 1)
            sub(0, 2)
            sub(1, 2)
            P.dma("sp", mixT_r[:, :, g0:g0 + MT], mst[tb], own=b_mst[tb], reads=[b_mst[tb]])
        P.barrier()
        block = stack.enter_context(nc.Block())
        first = [(bc[n].dsem, bc[n].dcnt) for n in bc]
        for e in ("pe", "act", "dve"):
            P.q[e].insert(0, (first, None, None, 0))
        P.emit(block)
    return nc


def build_F(last):
    nc = bass.Bass("TRN2", target_bir_lowering=False)
    hin_d = nc.dram_tensor("hin", [TF, D], F32, kind="ExternalInput").ap()
    mixT_d = nc.dram_tensor("mixT", [D, TF], BF16, kind="ExternalInput").ap()
    wo_d = nc.dram_tensor("wo", [8 * 128, NCC * 512], F32, kind="ExternalInput").ap()
    wu_d = nc.dram_tensor("wu", [2 * NFC * 128, NCC * 128], F32, kind="ExternalInput").ap()
    wd_d = nc.dram_tensor("wd", [8 * 128, NFC * 512], F32, kind="ExternalInput").ap()
    cw_d = nc.dram_tensor("cw", [128, 2 * NFC * 3], F32, kind="ExternalInput").ap()
    cb_d = nc.dram_tensor("cb", [128, 2 * NFC], F32, kind="ExternalInput").ap()
    gf_d = nc.dram_tensor("gffn", [128, NCC], F32, kind="ExternalInput").ap()
    flag_d = nc.dram_tensor("flag", [128, 1], F32, kind="ExternalInput").ap()
    idf_d = nc.dram_tensor("identf", [128, 128], F32, kind="ExternalInput").ap()
    if last:
        gb_d = nc.dram_tensor("gbc", [128, D], F32, kind="ExternalInput").ap()
        out_d = nc.dram_tensor("out", [TOKC, D], F32, kind="ExternalOutput").ap()
        h2_d = nc.dram_tensor("h2s", [TOKC, D], F32).ap()
    else:
        gn_d = nc.dram_tensor("gnext", [128, NCC], F32, kind="ExternalInput").ap()
        h2_d = nc.dram_tensor("hout", [TOKC, D], F32, kind="ExternalOutput").ap()
        xTn_d = nc.dram_tensor("xTn", [D, TOKC], BF16, kind="ExternalOutput").ap()
        xTn_r = xTn_d.rearrange("(cc p) t -> p cc t", p=128)
    h1_d = nc.dram_tensor("h1s", [TF, D], F32).ap()
    mixT_r = mixT_d.rearrange("(cc p) t -> p cc t", p=128)
    wo_r = wo_d.rearrange("(m p) (f n) -> m p f n", p=128, n=512)
    wu_r = wu_d.rearrange("(f p) (c n) -> f p c n", p=128, n=128)
    wd_r = wd_d.rearrange("(m p) (f n) -> m p f n", p=128, n=512)

    AT_B = NFC * TP * 2
    XT_B = NCC * TP * 2
    W_B = 4 * NCC * 128 * 2
    X_B = 10 * TP * 4
    C_B = 6 * 1024
    SB = AT_B + XT_B + W_B + X_B + C_B
    with contextlib.ExitStack() as stack:
        arena_t = stack.enter_context(nc.sbuf_tensor("arena", [128, SB], U8))
        ps_t = stack.enter_context(nc.psum_tensor("ps", [128, 4096], F32))
        ps = ps_t[:, :]
        A = Arena(arena_t, SB)
        P = Prog(nc, stack)
        o_at, o_xt, o_w, o_x, o_c = 0, AT_B, AT_B + XT_B, AT_B + XT_B + W_B, AT_B + XT_B + W_B + X_B
        aT = A.at(o_at, AT_B, BF16, "p (f t) -> p f t", f=NFC)
        xT = A.at(o_xt, XT_B, BF16, "p (c t) -> p c t", c=NCC)
        A.off = o_c
        cw = A.take(2 * NFC * 3 * 4, F32, "p (f k) -> p f k", k=3)
        cb = A.take(2 * NFC * 4, F32)
        gffn = A.take(NCC * 4, F32)
        gnx = A.take(NCC * 4, F32)
        flag = A.take(32, F32)
        identf = A.take(128 * 4, F32)
        st = [A.take(16, F32) for _ in range(2)]
        assert A.off <= SB
        wA = [A.at(o_at + i * 8192, 8192, BF16, "p (f n) -> p f n", n=512) for i in range(2)]
        hc = [A.at(o_at + 16384 + i * 2048, 2048, F32) for i in range(3)]
        oc = [A.at(o_at + 22528 + i * 2048, 2048, F32) for i in range(3)]
        hrow = [A.at(o_at + 28672 + i * 16384, 16384, F32) for i in range(2)]
        junk = A.at(o_at + 61440, 8192, BF16)
        gbc = A.at(o_w, 16384, F32)
        wg = [A.at(o_w + i * 8192, 8192, BF16, "p (c n) -> p c n", n=128) for i in range(2)]
        wv = [A.at(o_w + 16384 + i * 8192, 8192, BF16, "p (c n) -> p c n", n=128) for i in range(2)]
        ctmp = [A.at(o_x + i * TP * 4, TP * 4, F32) for i in range(10)]
        wC = [A.at(o_xt + i * 8192, 8192, BF16, "p (f n) -> p f n", n=512) for i in range(2)]
        hcC = [A.at(o_xt + 16384 + i * 2048, 2048, F32) for i in range(3)]
        ocC = [A.at(o_xt + 22528 + i * 2048, 2048, F32) for i in range(3)]

        bank = lambda i: ps[:, i * 512:(i + 1) * 512]
        b_ps = [P.buf("ps%d" % i) for i in range(8)]
        bcn = {n: P.buf(n, const=True) for n in ("cw", "cb", "gffn", "gnx", "flag", "identf", "gbc")}
        P.dma("sp", cw, cw_d.rearrange("p (f k) -> p f k", k=3), own=bcn["cw"], writes=[bcn["cw"]])
        P.dma("sp", cb, cb_d, own=bcn["cb"], writes=[bcn["cb"]])
        P.dma("sp", gffn, gf_d, own=bcn["gffn"], writes=[bcn["gffn"]])
        if not last:
            P.dma("sp", gnx, gn_d, own=bcn["gnx"], writes=[bcn["gnx"]])
        P.dma("sp", flag[:, 0:1], flag_d, own=bcn["flag"], writes=[bcn["flag"]])
        P.dma("sp", identf, idf_d, own=bcn["identf"], writes=[bcn["identf"]])
        b_st = [P.buf("st%d" % i) for i in range(2)]

        b_xT = P.buf("xT")
        b_wA = [P.buf("wA%d" % i) for i in range(2)]
        b_hc = [P.buf("hc%d" % i) for i in range(3)]
        b_oc = [P.buf("oc%d" % i) for i in range(3)]
        b_hrowA = [P.buf("hrow%d" % i) for i in range(2)]
        b_junkA = P.buf("junk")
        b_wg = [P.buf("wg%d" % i) for i in range(2)]
        b_wv = [P.buf("wv%d" % i) for i in range(2)]
        b_ct = [P.buf("ct%d" % i) for i in range(10)]
        b_aT = P.buf("aT")
        b_wC = [P.buf("wC%d" % i) for i in range(2)]
        b_hcC = [P.buf("hcC%d" % i) for i in range(3)]
        b_ocC = [P.buf("ocC%d" % i) for i in range(3)]
        b_hrowD = [P.buf("hrowD%d" % i) for i in range(2)]
        b_junkD = P.buf("junkD")
        b_xo = P.buf("xTo")

        for ps_i in range(NPASS):
            r0 = ps_i * PASS
            tiles = [(0, HALO)] + [(HALO + 128 * i, 128) for i in range(4)]
            P.dma("sp", xT, mixT_r[:, :, r0:r0 + TP], own=b_xT, writes=[b_xT])
            b_h1 = {}
            wcnt = 0
            ccnt = 0
            itemsA = [(mc_, fg_) for mc_ in range(8) for fg_ in range(4)]

            def loadA(i):
                mc_, fg_ = itemsA[i]
                P.dma("pool", wA[i % 2], wo_r[mc_, :, fg_ * 8:(fg_ + 1) * 8, :], own=b_wA[i % 2], writes=[b_wA[i % 2]],
                      max_dma_last_dim=4096)
            loadA(0)
            for mc in range(8):
                for fg in range(4):
                    wb_ = wcnt % 2
                    wcnt += 1
                    if wcnt < len(itemsA):
                        loadA(wcnt)
                    for fi in range(8):
                        fc = fg * 8 + fi
                        for ti, (c0, n) in enumerate(tiles):
                            P.pe_group([_mm(bank(ti)[:n, :], xT[:, fc, c0:c0 + n], wA[wb_][:, fi, :], fc == 0, fc == NCC - 1)],
                                       reads=[b_xT, b_wA[wb_]], writes=[b_ps[ti]])
                for ti, (c0, n) in enumerate(tiles):
                    cbi = ccnt % 3
                    ccnt += 1
                    rr = r0 + c0
                    P.dma("sp", hc[cbi][:n], hin_d[rr:rr + n, mc * 512:(mc + 1) * 512], own=b_hc[cbi], writes=[b_hc[cbi]])
                    P.op("dve", lambda e, ti=ti, n=n, cbi=cbi: e.tensor_tensor(out=oc[cbi][:n], in0=bank(ti)[:n, :], in1=hc[cbi][:n],
                                                                             op=ALU.add),
                         reads=[b_ps[ti], b_hc[cbi]], writes=[b_oc[cbi]])
                    bh = P.buf("h1_%d_%d" % (ti, mc))
                    b_h1[(ti, mc)] = bh
                    P.dma("sp", h1_d[rr:rr + n, mc * 512:(mc + 1) * 512], oc[cbi][:n], own=b_oc[cbi], reads=[b_oc[cbi]], writes=[bh])
            b_hrow = b_hrowA
            b_junk = b_junkA
            psb = [(i, b_ps[i]) for i in range(5, 8)]
            pscur = [0]
            for ti, (c0, n) in enumerate(tiles):
                hb = ti % 2
                rr = r0 + c0
                P.dma("sp", hrow[hb][:n], h1_d[rr:rr + n, :], own=b_hrow[hb], reads=[b_h1[(ti, mc)] for mc in range(8)],
                      writes=[b_hrow[hb]])
                emit_norm_T(P, hrow[hb], b_hrow[hb], n, st[hb], b_st[hb], junk, b_junk, gffn, identf, ps, psb,
                            xT, b_xT, c0, pscur)
            P.barrier()
            SEG = TP // 2
            def loadB(f_):
                P.dma("pool", wg[f_ % 2], wu_r[f_], own=b_wg[f_ % 2], writes=[b_wg[f_ % 2]], max_dma_last_dim=4096)
                P.dma("pool", wv[f_ % 2], wu_r[NFC + f_], own=b_wv[f_ % 2], writes=[b_wv[f_ % 2]], max_dma_last_dim=4096)
            loadB(0)
            for fc in range(NFC):
                wb_ = fc % 2
                if fc + 1 < NFC:
                    loadB(fc + 1)
                pb = 4 * (fc % 2)
                us = []
                for gi, (wt, bw) in enumerate(((wg[wb_], b_wg[wb_]), (wv[wb_], b_wv[wb_]))):
                    b0 = pb + 2 * gi
                    u = ps[:, b0 * 512 + 512 - SEG: b0 * 512 + 512 + SEG]
                    for sgi in range(2):
                        o = u[:, sgi * SEG:(sgi + 1) * SEG]
                        P.pe_group([_mm(o, wt[:, cc, :], xT[:, cc, sgi * SEG:(sgi + 1) * SEG], cc == 0, cc == NCC - 1)
                                    for cc in range(NCC)], reads=[b_xT, bw], writes=[b_ps[b0 + sgi]])
                    us.append((u, [b_ps[b0], b_ps[b0 + 1]]))
                cbase = 5 * (fc % 2)
                res = []
                for gi, (u, ub) in enumerate(us):
                    f = fc + gi * NFC
                    if ps_i == 0:
                        P.op("dve", lambda e, u=u: e.tensor_scalar(out=u[:, HALO - 2:HALO], in0=u[:, HALO - 2:HALO],
                                                                  scalar1=flag[:, 0:1], scalar2=None, op0=ALU.mult),
                             reads=ub + [bcn["flag"]], writes=ub)
                    t0, t1 = ctmp[cbase + 2 * gi], ctmp[cbase + 2 * gi + 1]
                    bt0, bt1 = b_ct[cbase + 2 * gi], b_ct[cbase + 2 * gi + 1]
                    P.op("act", lambda e, u=u, t0=t0, f=f: e.activation(out=t0[:, 2:TP], in_=u[:, 2:TP], func=AF.Identity,
                                                                       scale=cw[:, f, 2:3], bias=cb[:, f:f + 1]),
                         reads=ub + [bcn["cw"], bcn["cb"]], writes=[bt0])
                    P.op("dve", lambda e, u=u, t0=t0, t1=t1, f=f: e.scalar_tensor_tensor(out=t1[:, 2:TP], in0=u[:, 1:TP - 1],
                                                                                      scalar=cw[:, f, 1:2], in1=t0[:, 2:TP],
                                                                                      op0=ALU.mult, op1=ALU.add),
                         reads=ub + [bt0], writes=[bt1])
                    P.op("dve", lambda e, u=u, t0=t0, t1=t1, f=f: e.scalar_tensor_tensor(out=t0[:, 2:TP], in0=u[:, 0:TP - 2],
                                                                                      scalar=cw[:, f, 0:1], in1=t1[:, 2:TP],
                                                                                      op0=ALU.mult, op1=ALU.add),
                         reads=ub + [bt1], writes=[bt0])
                    res.append((t0, bt0))
                (tg, btg), (tv, btv) = res
                sgt, bsg = ctmp[cbase + 4], b_ct[cbase + 4]
                P.op("act", lambda e, tg=tg, sgt=sgt: e.activation(out=sgt[:, 2:TP], in_=tg[:, 2:TP], func=AF.Silu),
                     reads=[btg], writes=[bsg])
                P.op("pool", lambda e, fc=fc, sgt=sgt, tv=tv: e.tensor_tensor(out=aT[:, fc, 2:TP], in0=sgt[:, 2:TP], in1=tv[:, 2:TP],
                                                                            op=ALU.mult),
                     reads=[bsg, btv], writes=[b_aT])
                if ps_i == 0 and fc == 0:
                    P.op("pool", lambda e: e.memset(aT[:, :, 0:2], 0.0), writes=[b_aT])
            P.barrier()
            own_tiles = tiles[1:]
            b_h2 = {}
            wcnt = 0
            ccnt = 0
            NFG = (NFC + 7) // 8
            itemsC = [(mc_, fg_) for mc_ in range(8) for fg_ in range(NFG)]

            def loadC(i):
                mc_, fg_ = itemsC[i]
                nf_ = min(8, NFC - fg_ * 8)
                P.dma("pool", wC[i % 2][:, 0:nf_, :], wd_r[mc_, :, fg_ * 8:fg_ * 8 + nf_, :], own=b_wC[i % 2], writes=[b_wC[i % 2]],
                      max_dma_last_dim=4096)
            loadC(0)
            for mc in range(8):
                pbk = 4 * (mc % 2)
                for fg in range(NFG):
                    nf = min(8, NFC - fg * 8)
                    wb_ = wcnt % 2
                    wcnt += 1
                    if wcnt < len(itemsC):
                        loadC(wcnt)
                    for fi in range(nf):
                        fc = fg * 8 + fi
                        for ti, (c0, n) in enumerate(own_tiles):
                            P.pe_group([_mm(bank(pbk + ti), aT[:, fc, c0:c0 + n], wC[wb_][:, fi, :], fc == 0, fc == NFC - 1)],
                                       reads=[b_aT, b_wC[wb_]], writes=[b_ps[pbk + ti]])
                for ti, (c0, n) in enumerate(own_tiles):
                    cbi = ccnt % 3
                    ccnt += 1
                    rr = r0 + c0
                    ro = rr - HALO
                    P.dma("sp", hcC[cbi], h1_d[rr:rr + n, mc * 512:(mc + 1) * 512], own=b_hcC[cbi],
                          reads=[b_h1[(ti + 1, mc)]], writes=[b_hcC[cbi]])
                    P.op("dve", lambda e, ti=ti, cbi=cbi, pbk=pbk: e.tensor_tensor(out=ocC[cbi], in0=bank(pbk + ti), in1=hcC[cbi],
                                                                                 op=ALU.add),
                         reads=[b_ps[pbk + ti], b_hcC[cbi]], writes=[b_ocC[cbi]])
                    bh = P.buf("h2_%d_%d" % (ti, mc))
                    b_h2[(ti, mc)] = bh
                    P.dma("sp", h2_d[ro:ro + n, mc * 512:(mc + 1) * 512], ocC[cbi], own=b_ocC[cbi], reads=[b_ocC[cbi]], writes=[bh])
            P.barrier()
            b_hrow = b_hrowD
            b_junk = b_junkD
            psb = [(i, b_ps[i]) for i in range(8)]
            pscur = [0]
            if last:
                P.dma("sp", gbc, gb_d, own=bcn["gbc"], writes=[bcn["gbc"]])
            for ti, (c0, n) in enumerate(own_tiles):
                hb = ti % 2
                ro = r0 + c0 - HALO
                P.dma("sp", hrow[hb], h2_d[ro:ro + n, :], own=b_hrow[hb], reads=[b_h2[(ti, mc)] for mc in range(8)],
                      writes=[b_hrow[hb]])
                if last:
                    emit_rstd(P, hrow[hb], b_hrow[hb], n, D, junk, b_junk, st[hb], b_st[hb])
                    P.op("dve", lambda e, hb=hb: e.scalar_tensor_tensor(out=hrow[hb], in0=hrow[hb], scalar=st[hb][:, 3:4], in1=gbc,
                                                                       op0=ALU.mult, op1=ALU.mult),
                         reads=[b_hrow[hb], b_st[hb], bcn["gbc"]], writes=[b_hrow[hb]])
                    P.dma("sp", out_d[ro:ro + n, :], hrow[hb], own=b_hrow[hb], reads=[b_hrow[hb]])
                else:
                    emit_norm_T(P, hrow[hb], b_hrow[hb], n, st[hb], b_st[hb], junk, b_junk, gnx, identf, ps, psb,
                                xT, b_xo, c0, pscur)
            if not last:
                P.dma("sp", xTn_r[:, :, ps_i * PASS:(ps_i + 1) * PASS], xT[:, :, HALO:TP], own=b_xo, reads=[b_xo])
            P.barrier()
        block = stack.enter_context(nc.Block())
        first = [(b.dsem, b.dcnt) for n_, b in bcn.items() if b.dsem is not None and n_ != "gbc"]
        for e in ("pe", "act", "dve", "pool"):
            P.q[e].insert(0, (first, None, None, 0))
        P.emit(block)
    return nc


_CACHE = {}


def _prog(name):
    if name not in _CACHE:
        _CACHE[name] = {"N": build_N, "M": build_M, "F0": lambda: build_F(False), "F1": lambda: build_F(True)}[name]()
    return _CACHE[name]


def _fm(v):
    return np.ascontiguousarray(np.asarray(v, np.float32).reshape(NCC, 128).T)


def _consts():
    half = HD // 2
    inv = (1.0 / (np.float32(10000.0) ** np.linspace(0.0, 1.0, half, dtype=np.float32))).astype(np.float32)
    ang = (np.arange(SEQ, dtype=np.float32)[:, None] * inv[None, :]).astype(np.float32)
    cs = np.concatenate([np.cos(ang), np.sin(ang)], axis=1).astype(np.float32)
    identf = np.eye(128, dtype=np.float32)
    identb = np.eye(128, dtype=np.float32).astype(ml_dtypes.bfloat16)
    k = np.arange(128)[:, None, None]
    r = np.arange(5)[None, :, None]
    q = np.arange(128)[None, None, :]
    kp = (r - 4) * 128 + k
    dist = q - kp
    relidx = np.clip(dist, -63, 128) + 63
    kc = np.floor_divide(kp, 64)
    qc = q // 64
    valid = (kc <= qc) & (kc >= qc - 8)
    amask = np.where(valid, 0.0, NEG).astype(np.float32).reshape(128, 640)
    return cs, identf, identb, relidx, amask


def _ret_consts(g):
    dec = np.zeros((128, 8), np.float32)
    rmask = np.zeros((128, 2, 128), np.float32)
    p = np.arange(128, dtype=np.float64)
    for hh in range(2):
        h = 2 * g + hh
        lg = np.log(np.float32(1.0) - np.float32(2.0) ** np.float32(-5.0 - h)).astype(np.float64)
        dec[:, hh] = np.exp((p + 1.0) * lg)
        dec[:, 2 + hh] = np.exp((127.0 - p) * lg) * (HD ** -0.5)
        dec[:, 4 + hh] = np.exp(128.0 * lg)
        j = p[:, None]
        i = p[None, :]
        rmask[:, hh, :] = np.where(i >= j, np.exp(-128.0 * lg), 0.0)
    return dec, rmask.reshape(128, 256)


def _run(nc, in_maps):
    res = run_bass_kernel_spmd(nc, in_maps, core_ids=list(range(NCORES)))
    return res.results


def kernel(x, ln_mix, w_in, rel_bias, w_out, ln_ffn, w_up, conv_w, conv_b, w_down, ln_final):
    x = np.asarray(x, np.float32)
    cs, identf, identb, relidx, amask = _consts()
    depth = ln_mix.shape[0]
    h = x.reshape(TOK, D)
    gfm = _fm(ln_mix[0])
    res = _run(_prog("N"), [{"rows": h[c * TOKC:(c + 1) * TOKC], "gfm": gfm, "identf": identf} for c in range(NCORES)])
    xnT = np.concatenate([res[c]["xT"] for c in range(NCORES)], axis=1)
    out = None
    for l in range(depth):
        in_maps = []
        for g in range(NCORES):
            cols = []
            for blk in range(4):
                for hh in range(2):
                    c0 = blk * 2048 + (2 * g + hh) * 128
                    cols.append(np.arange(c0, c0 + 128))
            for blk in range(3):
                for hh in range(2):
                    c0 = 8192 + blk * 2048 + (2 * g + hh) * 128
                    cols.append(np.arange(c0, c0 + 128))
            cols = np.concatenate(cols)
            wsl = np.asarray(w_in[l])[:, cols]
            wsl = np.ascontiguousarray(wsl.reshape(NCC, 128, 1792).transpose(1, 0, 2)).reshape(128, NCC * 1792)
            dec, rmask = _ret_consts(g)
            bg = np.stack([np.asarray(rel_bias[l][2 * g + hh], np.float32)[relidx] for hh in range(2)], axis=1)
            in_maps.append({"xnT": xnT, "win": wsl, "cs": cs, "dec": dec, "rmask": rmask,
                            "biasg": np.ascontiguousarray(bg.reshape(128, 2 * 640)), "amask": amask, "identb": identb})
        res = _run(_prog("M"), in_maps)
        mixT = np.empty((D, TOK), dtype=ml_dtypes.bfloat16)
        for g in range(NCORES):
            m = res[g]["mixT"]
            mixT[(2 * g) * 128:(2 * g + 2) * 128] = m[0:256]
            mixT[2048 + (2 * g) * 128:2048 + (2 * g + 2) * 128] = m[256:512]
        del res
        last = (l == depth - 1)
        wo_t = np.ascontiguousarray(np.asarray(w_out[l]).reshape(NCC, 128, 8, 512).transpose(2, 1, 0, 3)).reshape(8 * 128, NCC * 512)
        wu_t = np.ascontiguousarray(np.asarray(w_up[l]).reshape(NCC, 128, 2 * NFC, 128).transpose(2, 1, 0, 3)).reshape(2 * NFC * 128, NCC * 128)
        wd_t = np.ascontiguousarray(np.asarray(w_down[l]).reshape(NFC, 128, 8, 512).transpose(2, 1, 0, 3)).reshape(8 * 128, NFC * 512)
        cw_t = np.ascontiguousarray(np.asarray(conv_w[l], np.float32).reshape(3, 2 * NFC, 128).transpose(2, 1, 0)).reshape(128, 2 * NFC * 3)
        cb_t = np.ascontiguousarray(np.asarray(conv_b[l], np.float32).reshape(2 * NFC, 128).T)
        gffn = _fm(ln_ffn[l])
        in_maps = []
        for c in range(NCORES):
            t0 = c * TOKC
            first = (t0 % SEQ == 0)
            hin = np.zeros((TF, D), np.float32)
            mx = np.zeros((D, TF), dtype=ml_dtypes.bfloat16)
            if first:
                hin[HALO:] = h[t0:t0 + TOKC]
                mx[:, HALO:] = mixT[:, t0:t0 + TOKC]
            else:
                hin[:] = h[t0 - HALO:t0 + TOKC]
                mx[:] = mixT[:, t0 - HALO:t0 + TOKC]
            im = {"hin": hin, "mixT": mx, "wo": wo_t, "wu": wu_t, "wd": wd_t, "cw": cw_t, "cb": cb_t, "gffn": gffn,
                  "flag": np.full((128, 1), 0.0 if first else 1.0, np.float32), "identf": identf}
            if last:
                im["gbc"] = np.ascontiguousarray(np.broadcast_to(np.asarray(ln_final, np.float32)[None, :], (128, D)))
            else:
                im["gnext"] = _fm(ln_mix[l + 1])
            in_maps.append(im)
        res = _run(_prog("F1" if last else "F0"), in_maps)
        del in_maps
        if last:
            out = np.concatenate([res[c]["out"] for c in range(NCORES)], axis=0)
        else:
            h = np.concatenate([res[c]["hout"] for c in range(NCORES)], axis=0)
            xnT = np.concatenate([res[c]["xTn"] for c in range(NCORES)], axis=1)
        del res
    return out.reshape(BATCH, SEQ, D).astype(np.float32)
```

```python
import contextlib
import numpy as np
import ml_dtypes
import concourse.bass as bass
import concourse.mybir as mybir
from concourse.bass_utils import run_bass_kernel_spmd

F32 = mybir.dt.float32
BF16 = mybir.dt.bfloat16
U8 = mybir.dt.uint8
AF = mybir.ActivationFunctionType
ALU = mybir.AluOpType

NCORES = 8
D = 4096
NCC = 32
DFF = 11008
NFC = 86
SEQ = 8192
BATCH = 2
TOK = BATCH * SEQ
TOKC = TOK // NCORES
HALO = 32
TF = TOKC + HALO
PASS = 512
TP = PASS + HALO
NPASS = TOKC // PASS
EPS = 1e-6
HD = 128
MT = 256
NEG = -30000.0


class Buf:
    __slots__ = ("name", "w", "r", "dsem", "dcnt", "const")

    def __init__(self, name, const=False):
        self.name = name
        self.w = None
        self.r = []
        self.dsem = None
        self.dcnt = 0
        self.const = const


class Prog:
    ENG = ("pe", "act", "dve", "pool", "sp")
    CENG = ("pe", "act", "dve", "pool")

    def __init__(self, nc, stack):
        self.nc = nc
        self.stack = stack
        self.q = {e: [] for e in self.ENG}
        self.seen = {e: {} for e in self.ENG}
        self.sem = {}
        self.cnt = {}
        for e in self.CENG:
            self.sem[e] = stack.enter_context(nc.semaphore("pg_" + e))
            self.cnt[e] = 0
        self.dma_bufs = []
        self.nsem = 0

    def buf(self, name, const=False):
        return Buf(name, const)

    def _dsem(self, b):
        if b.dsem is None:
            b.dsem = self.stack.enter_context(self.nc.semaphore("d%d" % self.nsem))
            self.nsem += 1
            self.dma_bufs.append(b)
        return b.dsem

    def _collect(self, eng, reads, writes, extra=()):
        w = list(extra)
        for b in reads:
            if b.w is not None:
                w.append(b.w)
        for b in writes:
            w.extend(b.r)
            if b.w is not None:
                w.append(b.w)
        best = {}
        for (s, v) in w:
            k = id(s)
            if k not in best or best[k][1] < v:
                best[k] = (s, v)
        out = []
        seen = self.seen[eng]
        for k, (s, v) in best.items():
            if eng == "pe" and s is self.sem["pe"]:
                continue
            if seen.get(k, -1) >= v:
                continue
            seen[k] = v
            out.append((s, v))
        return out

    @staticmethod
    def _addr(b, tok):
        if b.const:
            return
        for i, (s, v) in enumerate(b.r):
            if s is tok[0]:
                b.r[i] = tok
                return
        b.r.append(tok)

    def _record(self, tok, reads, writes):
        for b in reads:
            self._addr(b, tok)
        for b in writes:
            b.w = tok
            b.r = []

    def op(self, eng, fn, reads=(), writes=()):
        waits = self._collect(eng, reads, writes)
        self.cnt[eng] += 1
        tok = (self.sem[eng], self.cnt[eng])
        self.q[eng].append((waits, fn, tok, 1))
        self._record(tok, reads, writes)
        return tok

    def pe_group(self, fns, reads=(), writes=()):
        waits = self._collect("pe", reads, writes)
        self.cnt["pe"] += 1
        tok = (self.sem["pe"], self.cnt["pe"])
        n = len(fns)
        for i, fn in enumerate(fns):
            self.q["pe"].append((waits if i == 0 else [], fn, tok if i == n - 1 else None, 1))
        self._record(tok, reads, writes)
        return tok

    def dma(self, queue, out, in_, own, reads=(), writes=(), **kw):
        waits = self._collect(queue, reads, writes)
        sem = self._dsem(own)
        own.dcnt += 16
        tok = (sem, own.dcnt)
        self.q[queue].append((waits, (lambda e, o=out, i=in_, k=kw: e.dma_start(out=o, in_=i, **k)), tok, 16))
        self._record(tok, reads, writes)
        return tok

    def barrier(self):
        toks = [(self.sem[e], self.cnt[e]) for e in self.CENG if self.cnt[e] > 0]
        toks += [(b.dsem, b.dcnt) for b in self.dma_bufs if b.dcnt > 0]
        for e in self.ENG:
            waits = []
            seen = self.seen[e]
            for (s, v) in toks:
                if e == "pe" and s is self.sem["pe"]:
                    continue
                if seen.get(id(s), -1) >= v:
                    continue
                seen[id(s)] = v
                waits.append((s, v))
            if waits:
                self.q[e].append((waits, None, None, 0))

    def emit(self, block):
        def run(eng, items):
            for waits, fn, tok, inc in items:
                for (s, v) in waits:
                    eng.wait_ge(s, v)
                if fn is not None:
                    ins = fn(eng)
                    if tok is not None:
                        ins.then_inc(tok[0], inc)

        q = self.q

        @block.tensor
        def _(e):
            run(e, q["pe"])

        @block.scalar
        def _(e):
            run(e, q["act"])

        @block.vector
        def _(e):
            run(e, q["dve"])

        @block.gpsimd
        def _(e):
            run(e, q["pool"])

        @block.sync
        def _(e):
            run(e, q["sp"])


class Arena:
    def __init__(self, tensor, nbytes):
        self.t = tensor
        self.n = nbytes
        self.off = 0

    def at(self, off, nbytes, dt, pat=None, **kw):
        assert off + nbytes <= self.n, (off, nbytes, self.n)
        ap = self.t[:, off:off + nbytes]
        if dt is not U8:
            ap = ap.bitcast(dt)
        if pat:
            ap = ap.rearrange(pat, **kw)
        return ap

    def take(self, nbytes, dt, pat=None, **kw):
        nb = (nbytes + 31) // 32 * 32
        ap = self.at(self.off, nbytes, dt, pat, **kw)
        self.off += nb
        return ap


def _mm(out, lhsT, rhs, start, stop):
    return lambda e: e.matmul(out, lhsT, rhs, start=start, stop=stop)


def _tr(out, in_, ident):
    return lambda e: e.transpose(out, in_, ident)


def emit_rstd(P, src, srcbuf, n, width, junk, junkbuf, st, stbuf):
    P.op("act", lambda e: e.activation(out=junk[:n], in_=src[:n], func=AF.Square, accum_out=st[:n, 0:1]),
         reads=[srcbuf], writes=[junkbuf, stbuf])
    P.op("dve", lambda e: e.tensor_scalar(out=st[:n, 1:2], in0=st[:n, 0:1], scalar1=1.0 / width, scalar2=EPS,
                                          op0=ALU.mult, op1=ALU.add), reads=[stbuf], writes=[stbuf])
    P.op("act", lambda e: e.activation(out=st[:n, 2:3], in_=st[:n, 1:2], func=AF.Sqrt), reads=[stbuf], writes=[stbuf])
    P.op("dve", lambda e: e.reciprocal(out=st[:n, 3:4], in_=st[:n, 2:3]), reads=[stbuf], writes=[stbuf])


def emit_norm_T(P, hrow, hbuf, n, st, stbuf, junk, junkbuf, gfm, identf, ps, psbufs, dst, dstbuf, col0, pscur):
    emit_rstd(P, hrow, hbuf, n, D, junk, junkbuf, st, stbuf)
    P.op("dve", lambda e: e.tensor_scalar(out=hrow[:n], in0=hrow[:n], scalar1=st[:n, 3:4], scalar2=None, op0=ALU.mult),
         reads=[hbuf, stbuf], writes=[hbuf])
    for g4 in range(NCC // 4):
        bi, bb = psbufs[pscur[0] % len(psbufs)]
        pscur[0] += 1
        bank = ps[:, bi * 512:(bi + 1) * 512]
        fns = []
        for k in range(4):
            cc = g4 * 4 + k
            fns.append(_tr(bank[:, k * 128:k * 128 + n], hrow[:n, cc * 128:(cc + 1) * 128], identf[:n, :n]))
        P.pe_group(fns, reads=[hbuf], writes=[bb])
        for k in range(4):
            cc = g4 * 4 + k
            o = dst[:, cc, col0:col0 + n]
            i = bank[:, k * 128:k * 128 + n]
            if k % 2 == 0:
                P.op("act", lambda e, o=o, i=i, cc=cc: e.activation(out=o, in_=i, func=AF.Copy, scale=gfm[:, cc:cc + 1]),
                     reads=[bb], writes=[dstbuf])
            else:
                P.op("dve", lambda e, o=o, i=i, cc=cc: e.tensor_scalar(out=o, in0=i, scalar1=gfm[:, cc:cc + 1], scalar2=None,
                                                                   op0=ALU.mult), reads=[bb], writes=[dstbuf])


def build_N():
    nc = bass.Bass("TRN2", target_bir_lowering=False)
    rows = nc.dram_tensor("rows", [TOKC, D], F32, kind="ExternalInput").ap()
    gfm_d = nc.dram_tensor("gfm", [128, NCC], F32, kind="ExternalInput").ap()
    idf_d = nc.dram_tensor("identf", [128, 128], F32, kind="ExternalInput").ap()
    xT_d = nc.dram_tensor("xT", [D, TOKC], BF16, kind="ExternalOutput").ap()
    xT_r = xT_d.rearrange("(cc p) t -> p cc t", p=128)
    SB = 150 * 1024
    with contextlib.ExitStack() as stack:
        arena_t = stack.enter_context(nc.sbuf_tensor("arena", [128, SB], U8))
        ps_t = stack.enter_context(nc.psum_tensor("ps", [128, 4096], F32))
        ps = ps_t[:, :]
        A = Arena(arena_t, SB)
        P = Prog(nc, stack)
        gfm = A.take(NCC * 4, F32)
        identf = A.take(128 * 4, F32)
        st = [A.take(16, F32) for _ in range(2)]
        junk = A.take(D * 2, BF16)
        hrow = [A.take(D * 4, F32) for _ in range(2)]
        xst = [A.take(NCC * 512 * 2, BF16, "p (c t) -> p c t", c=NCC) for _ in range(2)]
        b_c = P.buf("consts", const=True)
        b_st = [P.buf("st%d" % i) for i in range(2)]
        b_junk = P.buf("junk")
        b_h = [P.buf("h%d" % i) for i in range(2)]
        b_x = [P.buf("x%d" % i) for i in range(2)]
        psb = [(i, P.buf("ps%d" % i)) for i in range(8)]
        pscur = [0]
        P.dma("sp", gfm, gfm_d, own=b_c, writes=[b_c])
        b_c2 = P.buf("consts2", const=True)
        P.dma("sp", identf, idf_d, own=b_c2, writes=[b_c2])
        b_c.const = True
        ntile = TOKC // 128
        for i in range(ntile):
            hb = i % 2
            P.dma("sp", hrow[hb], rows[i * 128:(i + 1) * 128, :], own=b_h[hb], writes=[b_h[hb]])
            xs = (i // 4) % 2
            if i == 0:
                for e in ("act", "dve", "pe"):
                    P.seen[e]
            emit_norm_T(P, hrow[hb], b_h[hb], 128, st[hb], b_st[hb], junk, b_junk, gfm, identf, ps, psb,
                        xst[xs], b_x[xs], (i % 4) * 128, pscur)
            if i % 4 == 3:
                t0 = (i // 4) * 512
                P.dma("sp", xT_r[:, :, t0:t0 + 512], xst[xs], own=b_x[xs], reads=[b_x[xs]])
        P.barrier()
        block = stack.enter_context(nc.Block())
        for e in ("pe", "act", "dve"):
            P.q[e].insert(0, ([(b_c.dsem, 16), (b_c2.dsem, 16)], None, None, 0))
        P.emit(block)
    return nc


def build_M():
    nc = bass.Bass("TRN2", target_bir_lowering=False)
    xnT_d = nc.dram_tensor("xnT", [D, TOK], BF16, kind="ExternalInput").ap()
    win_d = nc.dram_tensor("win", [128, NCC * 1792], F32, kind="ExternalInput").ap()
    cs_d = nc.dram_tensor("cs", [SEQ, 128], F32, kind="ExternalInput").ap()
    dec_d = nc.dram_tensor("dec", [128, 8], F32, kind="ExternalInput").ap()
    rmask_d = nc.dram_tensor("rmask", [128, 256], F32, kind="ExternalInput").ap()
    biasg_d = nc.dram_tensor("biasg", [128, 2 * 640], F32, kind="ExternalInput").ap()
    amask_d = nc.dram_tensor("amask", [128, 640], F32, kind="ExternalInput").ap()
    idb_d = nc.dram_tensor("identb", [128, 128], BF16, kind="ExternalInput").ap()
    mixT_d = nc.dram_tensor("mixT", [512, TOK], BF16, kind="ExternalOutput").ap()
    xnT_r = xnT_d.rearrange("(cc p) t -> p cc t", p=128)
    mixT_r = mixT_d.rearrange("(k p) t -> p k t", p=128)
    SB = 200 * 1024
    scale = float(HD) ** -0.5
    with contextlib.ExitStack() as stack:
        arena_t = stack.enter_context(nc.sbuf_tensor("arena", [128, SB], U8))
        ps_t = stack.enter_context(nc.psum_tensor("ps", [128, 4096], F32))
        ps = ps_t[:, :]
        A = Arena(arena_t, SB)
        P = Prog(nc, stack)
        win = A.take(NCC * 1792 * 2, BF16, "p (c f) -> p c f", c=NCC)
        xt = [A.take(NCC * MT * 2, BF16, "p (c t) -> p c t", c=NCC) for _ in range(2)]
        dec = A.take(8 * 4, F32)
        rmask = A.take(256 * 4, F32, "p (h i) -> p h i", h=2)
        biasf = A.take(2 * 640 * 4, F32, "p (h k) -> p h k", h=2)
        amask = A.take(640 * 4, F32)
        identb = A.take(128 * 2, BF16)
        cst = [A.take(128 * 4, F32) for _ in range(2)]
        rot = A.take(512 * 4, F32, "p (j h d) -> p j h d", j=4, h=2)
        tmpa = A.take(256 * 4, F32, "p (j d) -> p j d", j=4)
        tmpb = A.take(256 * 4, F32, "p (j d) -> p j d", j=4)
        qkb2 = [A.take(512 * 2, BF16, "p (j d) -> p j d", j=4) for _ in range(2)]
        qkT2 = [A.take(512 * 2, BF16, "p (j d) -> p j d", j=4) for _ in range(2)]
        vb2 = [A.take(256 * 2, BF16, "p (h d) -> p h d", h=2) for _ in range(2)]
        sg2 = [A.take(256 * 4, F32, "p (h d) -> p h d", h=2) for _ in range(2)]
        stm = A.take(256 * 2, BF16, "p (h d) -> p h d", h=2)
        state = A.take(256 * 4, F32, "p (h d) -> p h d", h=2)
        stateb = A.take(256 * 2, BF16, "p (h d) -> p h d", h=2)
        junk = A.take(128 * 4, F32)
        rst = [A.take(16, F32) for _ in range(2)]
        rob = A.take(256 * 2, BF16, "p (h d) -> p h d", h=2)
        aob = A.take(256 * 2, BF16, "p (h d) -> p h d", h=2)
        NSLOT = 8
        akT = A.take(2 * NSLOT * 128 * 2, BF16, "p (h s d) -> p h s d", h=2, s=NSLOT)
        avr = A.take(2 * NSLOT * 132 * 2, BF16, "p (h s d) -> p h s d", h=2, s=NSLOT)
        aqT = [A.take(2 * MT * 2, BF16, "p (h t) -> p h t", h=2) for _ in range(2)]
        ssb = [A.take(640 * 4, F32) for _ in range(2)]
        ptb = [A.take(640 * 2, BF16) for _ in range(2)]
        rinv = [A.take(16, F32) for _ in range(2)]
        mst = [A.take(4 * MT * 2, BF16, "p (k t) -> p k t", k=4) for _ in range(2)]

        bc = {}
        for nm in ("win", "dec", "rmask", "biasg", "amask", "identb"):
            bc[nm] = P.buf(nm, const=True)
        b_xt = [P.buf("xt%d" % i) for i in range(2)]
        b_cs = [P.buf("cs%d" % i) for i in range(2)]
        b_rot, b_ta, b_tb = (P.buf(n) for n in ("rot", "ta", "tb"))
        b_qkb2 = [P.buf("qkb%d" % i) for i in range(2)]
        b_qkT2 = [P.buf("qkT%d" % i) for i in range(2)]
        b_vb2 = [P.buf("vb%d" % i) for i in range(2)]
        b_sg2 = [P.buf("sg%d" % i) for i in range(2)]
        b_stm = [P.buf("stm%d" % h) for h in range(2)]
        b_state = [P.buf("state%d" % h) for h in range(2)]
        b_stateb = [P.buf("stateb%d" % h) for h in range(2)]
        b_junk = P.buf("junk")
        b_rst = [P.buf("rst%d" % h) for h in range(2)]
        b_rob = [P.buf("rob%d" % h) for h in range(2)]
        b_aob = [P.buf("aob%d" % h) for h in range(2)]
        b_akT = [[P.buf("akT%d_%d" % (h, s)) for s in range(NSLOT)] for h in range(2)]
        b_avr = [[P.buf("avr%d_%d" % (h, s)) for s in range(NSLOT)] for h in range(2)]
        b_aqT = [P.buf("aqT%d" % i) for i in range(2)]
        b_ssb = [P.buf("ssb%d" % i) for i in range(2)]
        b_ptb = [P.buf("ptb%d" % i) for i in range(2)]
        b_rinv = [P.buf("rinv%d" % i) for i in range(2)]
        b_mst = [P.buf("mst%d" % i) for i in range(2)]
        bank = lambda i: ps[:, i * 512:(i + 1) * 512]
        psA, psB, psC = bank(0), bank(1), bank(2)
        psF = [bank(3), bank(4)]
        psT = bank(5).bitcast(BF16)
        psR = bank(6)
        psS = bank(7)
        b_psA, b_psB, b_psC = P.buf("psA"), P.buf("psB"), P.buf("psC")
        b_psF = [P.buf("psF0"), P.buf("psF1")]
        b_psT1, b_psT2 = P.buf("psT1"), P.buf("psT2")
        b_psRs = [P.buf("psRs%d" % h) for h in range(2)]
        b_psRo = [P.buf("psRo%d" % h) for h in range(2)]
        b_psSu = [P.buf("psSu%d" % h) for h in range(2)]
        b_psPV = [P.buf("psPV%d" % h) for h in range(2)]
        psPV = [psC[:, 256:256 + 129], psS[:, 256:256 + 129]]

        for c4 in range(8):
            P.dma("pool", win[:, c4 * 4:(c4 + 1) * 4, :],
                  win_d[:, c4 * 4 * 1792:(c4 + 1) * 4 * 1792].rearrange("p (c f) -> p c f", c=4),
                  own=bc["win"], writes=[bc["win"]], max_dma_last_dim=4096)
        P.dma("sp", dec, dec_d, own=bc["dec"], writes=[bc["dec"]])
        P.dma("sp", rmask, rmask_d.rearrange("p (h i) -> p h i", h=2), own=bc["rmask"], writes=[bc["rmask"]])
        P.dma("sp", biasf, biasg_d.rearrange("p (h k) -> p h k", h=2), own=bc["biasg"], writes=[bc["biasg"]])
        P.dma("sp", amask, amask_d, own=bc["amask"], writes=[bc["amask"]])
        P.dma("sp", identb, idb_d, own=bc["identb"], writes=[bc["identb"]])
        b_bias = P.buf("bias")
        for h in range(2):
            P.op("dve", lambda e, h=h: e.tensor_tensor(out=biasf[:, h, :], in0=biasf[:, h, :], in1=amask, op=ALU.add),
                 reads=[bc["biasg"], bc["amask"]], writes=[b_bias])
        b_bias.const = True
        b_ones = P.buf("ones")
        P.op("dve", lambda e: e.memset(avr[:, :, :, 128:129], 1.0), writes=[b_ones])
        b_ones.const = True

        ntile = TOK // MT
        tiles_per_seq = SEQ // MT
        P.dma("sp", xt[0], xnT_r[:, :, 0:MT], own=b_xt[0], writes=[b_xt[0]])
        for T in range(ntile):
            g0 = T * MT
            tb = T % 2
            Ts = T % tiles_per_seq
            if T + 1 < ntile:
                P.dma("sp", xt[1 - tb], xnT_r[:, :, g0 + MT:g0 + 2 * MT], own=b_xt[1 - tb], writes=[b_xt[1 - tb]])
            if Ts == 0:
                for h in range(2):
                    P.op("dve", lambda e, h=h: e.memset(state[:, h, :], 0.0), writes=[b_state[h]])
                    P.op("act", lambda e, h=h: e.copy(out=stateb[:, h, :], in_=state[:, h, :]),
                         reads=[b_state[h]], writes=[b_stateb[h]])
            for k in range(4):
                pf = psF[k // 2]
                o = pf[:, (k % 2) * MT:(k % 2 + 1) * MT]
                fns = [_mm(o, win[:, cc, 1024 + k * 128:1024 + (k + 1) * 128], xt[tb][:, cc, :], cc == 0, cc == NCC - 1)
                       for cc in range(NCC)]
                P.pe_group(fns, reads=[b_xt[tb], bc["win"]], writes=[b_psF[k // 2]])
            for h in range(2):
                P.op("act", lambda e, h=h, tb=tb: e.activation(out=aqT[tb][:, h, :], in_=psF[0][:, h * MT:(h + 1) * MT],
                                                        func=AF.Copy, scale=scale), reads=[b_psF[0]], writes=[b_aqT[tb]])
            for s in range(2):
                m = 2 * Ts + s
                slot = m % NSLOT
                for h in range(2):
                    P.op("dve", lambda e, h=h, s=s, slot=slot: e.tensor_copy(
                        out=akT[:, h, slot, :], in_=psF[1][:, h * MT + s * 128:h * MT + (s + 1) * 128]),
                        reads=[b_psF[1]], writes=[b_akT[h][slot]])
            def sub(s, phase, T=T, tb=tb, Ts=Ts, g0=g0):
                qkb, qkT, vb, sg = qkb2[s], qkT2[s], vb2[s], sg2[s]
                b_qkb, b_qkT, b_vb, b_sg = b_qkb2[s], b_qkT2[s], b_vb2[s], b_sg2[s]
                m = 2 * Ts + s
                slot = m % NSLOT
                pos = Ts * MT + s * 128
                cb_ = (2 * T + s) % 2
                cs_t = cst[cb_]
                if phase == 1:
                    P.dma("sp", cs_t, cs_d[pos:pos + 128, :], own=b_cs[cb_], writes=[b_cs[cb_]])
                    lhs = lambda cc: xt[tb][:, cc, s * 128:(s + 1) * 128]
                    P.pe_group([_mm(psA, lhs(cc), win[:, cc, 0:512], cc == 0, cc == NCC - 1) for cc in range(NCC)],
                               reads=[b_xt[tb], bc["win"]], writes=[b_psA])
                    P.pe_group([_mm(psB, lhs(cc), win[:, cc, 512:1024], cc == 0, cc == NCC - 1) for cc in range(NCC)],
                               reads=[b_xt[tb]], writes=[b_psB])
                    P.pe_group([_mm(psC[:, 0:256], lhs(cc), win[:, cc, 1536:1792], cc == 0, cc == NCC - 1) for cc in range(NCC)],
                               reads=[b_xt[tb]], writes=[b_psC])
                    pa = psA.rearrange("p (j h d) -> p j h d", j=4, h=2)
                    cosb = cs_t[:, 0:64].unsqueeze(1).broadcast_to([128, 4, 64])
                    sinb = cs_t[:, 64:128].unsqueeze(1).broadcast_to([128, 4, 64])
                    x1, x2 = pa[:, :, 0, :], pa[:, :, 1, :]
                    P.op("dve", lambda e, x1=x1, cosb=cosb: e.tensor_tensor(out=tmpa, in0=x1, in1=cosb, op=ALU.mult),
                         reads=[b_psA, b_cs[cb_]], writes=[b_ta])
                    P.op("dve", lambda e, x2=x2, sinb=sinb: e.tensor_tensor(out=tmpb, in0=x2, in1=sinb, op=ALU.mult),
                         reads=[b_psA, b_cs[cb_]], writes=[b_tb])
                    P.op("dve", lambda e: e.tensor_tensor(out=rot[:, :, 0, :], in0=tmpa, in1=tmpb, op=ALU.subtract),
                         reads=[b_ta, b_tb], writes=[b_rot])
                    P.op("dve", lambda e, x1=x1, sinb=sinb: e.tensor_tensor(out=tmpa, in0=x1, in1=sinb, op=ALU.mult),
                         reads=[b_psA, b_cs[cb_]], writes=[b_ta])
                    P.op("dve", lambda e, x2=x2, cosb=cosb: e.tensor_tensor(out=tmpb, in0=x2, in1=cosb, op=ALU.mult),
                         reads=[b_psA, b_cs[cb_]], writes=[b_tb])
                    P.op("dve", lambda e: e.tensor_tensor(out=rot[:, :, 1, :], in0=tmpa, in1=tmpb, op=ALU.add),
                         reads=[b_ta, b_tb], writes=[b_rot])
                    for j in range(4):
                        P.op("act", lambda e, j=j: e.activation(out=qkb[:, j, :], in_=rot[:, j, :, :].rearrange("p h d -> p (h d)"),
                                                                func=AF.Copy, scale=dec[:, j:j + 1]),
                             reads=[b_rot, bc["dec"]], writes=[b_qkb])
                    P.pe_group([_tr(psT[:, j * 128:(j + 1) * 128], qkb[:, j, :], identb) for j in range(4)],
                               reads=[b_qkb, bc["identb"]], writes=[b_psT1])
                    P.op("dve", lambda e: e.tensor_copy(out=qkT.rearrange("p j d -> p (j d)"), in_=psT[:, 0:512]),
                         reads=[b_psT1], writes=[b_qkT])
                    P.op("act", lambda e: e.copy(out=vb.rearrange("p h d -> p (h d)"), in_=psB[:, 0:256]), reads=[b_psB], writes=[b_vb])
                    P.op("act", lambda e: e.activation(out=sg.rearrange("p h d -> p (h d)"), in_=psB[:, 256:512], func=AF.Silu),
                         reads=[b_psB], writes=[b_sg])
                    for h in range(2):
                        P.op("act", lambda e, h=h, slot=slot: e.copy(out=avr[:, h, slot, 0:128], in_=psC[:, h * 128:(h + 1) * 128]),
                             reads=[b_psC], writes=[b_avr[h][slot]])
                else:
                    mb = (2 * T + s) % 2
                    for h in range(2):
                        P.pe_group([_mm(psR[:, h * 128:(h + 1) * 128], qkT[:, 2 + h, :], qkT[:, h, :], True, True)],
                                   reads=[b_qkT], writes=[b_psRs[h]])
                        P.op("dve", lambda e, h=h: e.tensor_tensor(out=stm[:, h, :], in0=psR[:, h * 128:(h + 1) * 128],
                                                                   in1=rmask[:, h, :], op=ALU.mult),
                             reads=[b_psRs[h], bc["rmask"]], writes=[b_stm[h]])
                        po = psR[:, 256 + h * 128:256 + (h + 1) * 128]
                        P.pe_group([_mm(po, stm[:, h, :], vb[:, h, :], True, False),
                                    _mm(po, qkT[:, h, :], stateb[:, h, :], False, True)],
                                   reads=[b_stm[h], b_vb, b_qkT, b_stateb[h]], writes=[b_psRo[h]])
                        pu = psS[:, h * 128:(h + 1) * 128]
                        P.pe_group([_mm(pu, qkb[:, 2 + h, :], vb[:, h, :], True, True)], reads=[b_qkb, b_vb], writes=[b_psSu[h]])
                        P.op("dve", lambda e, h=h, pu=pu: e.scalar_tensor_tensor(out=state[:, h, :], in0=state[:, h, :],
                                                                               scalar=dec[:, 4 + h:5 + h], in1=pu,
                                                                               op0=ALU.mult, op1=ALU.add),
                             reads=[b_state[h], b_psSu[h], bc["dec"]], writes=[b_state[h]])
                        P.op("act", lambda e, h=h: e.copy(out=stateb[:, h, :], in_=state[:, h, :]),
                             reads=[b_state[h]], writes=[b_stateb[h]])
                        r_ = rst[h]
                        P.op("act", lambda e, po=po, r_=r_: e.activation(out=junk, in_=po, func=AF.Square, accum_out=r_[:, 0:1]),
                             reads=[b_psRo[h]], writes=[b_junk, b_rst[h]])
                        P.op("dve", lambda e, r_=r_: e.tensor_scalar(out=r_[:, 1:2], in0=r_[:, 0:1], scalar1=1.0 / HD, scalar2=EPS,
                                                                    op0=ALU.mult, op1=ALU.add), reads=[b_rst[h]], writes=[b_rst[h]])
                        P.op("act", lambda e, r_=r_: e.activation(out=r_[:, 2:3], in_=r_[:, 1:2], func=AF.Sqrt),
                             reads=[b_rst[h]], writes=[b_rst[h]])
                        P.op("dve", lambda e, r_=r_: e.reciprocal(out=r_[:, 3:4], in_=r_[:, 2:3]), reads=[b_rst[h]], writes=[b_rst[h]])
                        P.op("dve", lambda e, h=h, po=po, r_=r_: e.scalar_tensor_tensor(out=rob[:, h, :], in0=po, scalar=r_[:, 3:4],
                                                                                      in1=sg[:, h, :], op0=ALU.mult, op1=ALU.mult),
                             reads=[b_psRo[h], b_rst[h], b_sg], writes=[b_rob[h]])
                    for h in range(2):
                        ab = (2 * (2 * T + s) + h) % 2
                        r0 = max(0, 4 - m)
                        fx = psF[0]
                        fy = psF[1]
                        for r in range(r0, 5):
                            kslot = (m - 4 + r) % NSLOT
                            o = fx[:, r * 128:(r + 1) * 128] if r < 4 else fy[:, 0:128]
                            P.pe_group([_mm(o, akT[:, h, kslot, :], aqT[tb][:, h, s * 128:(s + 1) * 128], True, True)],
                                       reads=[b_akT[h][kslot], b_aqT[tb]], writes=[b_psF[0] if r < 4 else b_psF[1]])
                        if r0 < 4:
                            P.op("dve", lambda e, h=h, ab=ab, r0=r0: e.tensor_tensor(
                                out=ssb[ab][:, r0 * 128:512], in0=psF[0][:, r0 * 128:512], in1=biasf[:, h, r0 * 128:512], op=ALU.add),
                                reads=[b_psF[0], b_bias], writes=[b_ssb[ab]])
                        P.op("dve", lambda e, h=h, ab=ab: e.tensor_tensor(
                            out=ssb[ab][:, 512:640], in0=psF[1][:, 0:128], in1=biasf[:, h, 512:640], op=ALU.add),
                            reads=[b_psF[1], b_bias], writes=[b_ssb[ab]])
                        P.op("act", lambda e, ab=ab, r0=r0: e.activation(out=ptb[ab][:, r0 * 128:640], in_=ssb[ab][:, r0 * 128:640],
                                                                        func=AF.Exp), reads=[b_ssb[ab]], writes=[b_ptb[ab]])
                        fns = []
                        rd = [b_ptb[ab], b_ones]
                        for r in range(r0, 5):
                            kslot = (m - 4 + r) % NSLOT
                            fns.append(_mm(psPV[h], ptb[ab][:, r * 128:(r + 1) * 128], avr[:, h, kslot, 0:129], r == r0, r == 4))
                            rd.append(b_avr[h][kslot])
                        P.pe_group(fns, reads=rd, writes=[b_psPV[h]])
                        P.op("dve", lambda e, h=h, ab=ab: e.reciprocal(out=rinv[ab][:, 0:1], in_=psPV[h][:, 128:129]),
                             reads=[b_psPV[h]], writes=[b_rinv[ab]])
                        P.op("dve", lambda e, h=h, ab=ab: e.tensor_scalar(out=aob[:, h, :], in0=psPV[h][:, 0:128],
                                                                         scalar1=rinv[ab][:, 0:1], scalar2=None, op0=ALU.mult),
                             reads=[b_psPV[h], b_rinv[ab]], writes=[b_aob[h]])
                    fns = [_tr(psT[:, (4 + h) * 128:(5 + h) * 128], rob[:, h, :], identb) for h in range(2)]
                    fns += [_tr(psT[:, (6 + h) * 128:(7 + h) * 128], aob[:, h, :], identb) for h in range(2)]
                    P.pe_group(fns, reads=[b_rob[0], b_rob[1], b_aob[0], b_aob[1]], writes=[b_psT2])
                    P.op("act", lambda e, tb=tb, s=s: e.copy(out=mst[tb][:, :, s * 128:(s + 1) * 128],
                                                            in_=psT[:, 512:1024].rearrange("p (k t) -> p k t", k=4)),
                         reads=[b_psT2], writes=[b_mst[tb]])

            sub(0, 1)
            sub(1, 1)
            sub(0, 2)
            sub(1, 2)
            P.dma("sp", mixT_r[:, :, g0:g0 + MT], mst[tb], own=b_mst[tb], reads=[b_mst[tb]])
        P.barrier()
        block = stack.enter_context(nc.Block())
        first = [(bc[n].dsem, bc[n].dcnt) for n in bc]
        for e in ("pe", "act", "dve"):
            P.q[e].insert(0, (first, None, None, 0))
        P.emit(block)
    return nc


def build_F(last):
    nc = bass.Bass("TRN2", target_bir_lowering=False)
    hin_d = nc.dram_tensor("hin", [TF, D], F32, kind="ExternalInput").ap()
    mixT_d = nc.dram_tensor("mixT", [D, TF], BF16, kind="ExternalInput").ap()
    wo_d = nc.dram_tensor("wo", [8 * 128, NCC * 512], F32, kind="ExternalInput").ap()
    wu_d = nc.dram_tensor("wu", [2 * NFC * 128, NCC * 128], F32, kind="ExternalInput").ap()
    wd_d = nc.dram_tensor("wd", [8 * 128, NFC * 512], F32, kind="ExternalInput").ap()
    cw_d = nc.dram_tensor("cw", [128, 2 * NFC * 3], F32, kind="ExternalInput").ap()
    cb_d = nc.dram_tensor("cb", [128, 2 * NFC], F32, kind="ExternalInput").ap()
    gf_d = nc.dram_tensor("gffn", [128, NCC], F32, kind="ExternalInput").ap()
    flag_d = nc.dram_tensor("flag", [128, 1], F32, kind="ExternalInput").ap()
    idf_d = nc.dram_tensor("identf", [128, 128], F32, kind="ExternalInput").ap()
    if last:
        gb_d = nc.dram_tensor("gbc", [128, D], F32, kind="ExternalInput").ap()
        out_d = nc.dram_tensor("out", [TOKC, D], F32, kind="ExternalOutput").ap()
        h2_d = nc.dram_tensor("h2s", [TOKC, D], F32).ap()
    else:
        gn_d = nc.dram_tensor("gnext", [128, NCC], F32, kind="ExternalInput").ap()
        h2_d = nc.dram_tensor("hout", [TOKC, D], F32, kind="ExternalOutput").ap()
        xTn_d = nc.dram_tensor("xTn", [D, TOKC], BF16, kind="ExternalOutput").ap()
        xTn_r = xTn_d.rearrange("(cc p) t -> p cc t", p=128)
    h1_d = nc.dram_tensor("h1s", [TF, D], F32).ap()
    mixT_r = mixT_d.rearrange("(cc p) t -> p cc t", p=128)
    wo_r = wo_d.rearrange("(m p) (f n) -> m p f n", p=128, n=512)
    wu_r = wu_d.rearrange("(f p) (c n) -> f p c n", p=128, n=128)
    wd_r = wd_d.rearrange("(m p) (f n) -> m p f n", p=128, n=512)

    AT_B = NFC * TP * 2
    XT_B = NCC * TP * 2
    W_B = 4 * NCC * 128 * 2
    X_B = 10 * TP * 4
    C_B = 6 * 1024
    SB = AT_B + XT_B + W_B + X_B + C_B
    with contextlib.ExitStack() as stack:
        arena_t = stack.enter_context(nc.sbuf_tensor("arena", [128, SB], U8))
        ps_t = stack.enter_context(nc.psum_tensor("ps", [128, 4096], F32))
        ps = ps_t[:, :]
        A = Arena(arena_t, SB)
        P = Prog(nc, stack)
        o_at, o_xt, o_w, o_x, o_c = 0, AT_B, AT_B + XT_B, AT_B + XT_B + W_B, AT_B + XT_B + W_B + X_B
        aT = A.at(o_at, AT_B, BF16, "p (f t) -> p f t", f=NFC)
        xT = A.at(o_xt, XT_B, BF16, "p (c t) -> p c t", c=NCC)
        A.off = o_c
        cw = A.take(2 * NFC * 3 * 4, F32, "p (f k) -> p f k", k=3)
        cb = A.take(2 * NFC * 4, F32)
        gffn = A.take(NCC * 4, F32)
        gnx = A.take(NCC * 4, F32)
        flag = A.take(32, F32)
        identf = A.take(128 * 4, F32)
        st = [A.take(16, F32) for _ in range(2)]
        assert A.off <= SB
        wA = [A.at(o_at + i * 8192, 8192, BF16, "p (f n) -> p f n", n=512) for i in range(2)]
        hc = [A.at(o_at + 16384 + i * 2048, 2048, F32) for i in range(3)]
        oc = [A.at(o_at + 22528 + i * 2048, 2048, F32) for i in range(3)]
        hrow = [A.at(o_at + 28672 + i * 16384, 16384, F32) for i in range(2)]
        junk = A.at(o_at + 61440, 8192, BF16)
        gbc = A.at(o_w, 16384, F32)
        wg = [A.at(o_w + i * 8192, 8192, BF16, "p (c n) -> p c n", n=128) for i in range(2)]
        wv = [A.at(o_w + 16384 + i * 8192, 8192, BF16, "p (c n) -> p c n", n=128) for i in range(2)]
        ctmp = [A.at(o_x + i * TP * 4, TP * 4, F32) for i in range(10)]
        wC = [A.at(o_xt + i * 8192, 8192, BF16, "p (f n) -> p f n", n=512) for i in range(2)]
        hcC = [A.at(o_xt + 16384 + i * 2048, 2048, F32) for i in range(3)]
        ocC = [A.at(o_xt + 22528 + i * 2048, 2048, F32) for i in range(3)]

        bank = lambda i: ps[:, i * 512:(i + 1) * 512]
        b_ps = [P.buf("ps%d" % i) for i in range(8)]
        bcn = {n: P.buf(n, const=True) for n in ("cw", "cb", "gffn", "gnx", "flag", "identf", "gbc")}
        P.dma("sp", cw, cw_d.rearrange("p (f k) -> p f k", k=3), own=bcn["cw"], writes=[bcn["cw"]])
        P.dma("sp", cb, cb_d, own=bcn["cb"], writes=[bcn["cb"]])
        P.dma("sp", gffn, gf_d, own=bcn["gffn"], writes=[bcn["gffn"]])
        if not last:
            P.dma("sp", gnx, gn_d, own=bcn["gnx"], writes=[bcn["gnx"]])
        P.dma("sp", flag[:, 0:1], flag_d, own=bcn["flag"], writes=[bcn["flag"]])
        P.dma("sp", identf, idf_d, own=bcn["identf"], writes=[bcn["identf"]])
        b_st = [P.buf("st%d" % i) for i in range(2)]

        b_xT = P.buf("xT")
        b_wA = [P.buf("wA%d" % i) for i in range(2)]
        b_hc = [P.buf("hc%d" % i) for i in range(3)]
        b_oc = [P.buf("oc%d" % i) for i in range(3)]
        b_hrowA = [P.buf("hrow%d" % i) for i in range(2)]
        b_junkA = P.buf("junk")
        b_wg = [P.buf("wg%d" % i) for i in range(2)]
        b_wv = [P.buf("wv%d" % i) for i in range(2)]
        b_ct = [P.buf("ct%d" % i) for i in range(10)]
        b_aT = P.buf("aT")
        b_wC = [P.buf("wC%d" % i) for i in range(2)]
        b_hcC = [P.buf("hcC%d" % i) for i in range(3)]
        b_ocC = [P.buf("ocC%d" % i) for i in range(3)]
        b_hrowD = [P.buf("hrowD%d" % i) for i in range(2)]
        b_junkD = P.buf("junkD")
        b_xo = P.buf("xTo")

        for ps_i in range(NPASS):
            r0 = ps_i * PASS
            tiles = [(0, HALO)] + [(HALO + 128 * i, 128) for i in range(4)]
            P.dma("sp", xT, mixT_r[:, :, r0:r0 + TP], own=b_xT, writes=[b_xT])
            b_h1 = {}
            wcnt = 0
            ccnt = 0
            itemsA = [(mc_, fg_) for mc_ in range(8) for fg_ in range(4)]

            def loadA(i):
                mc_, fg_ = itemsA[i]
                P.dma("pool", wA[i % 2], wo_r[mc_, :, fg_ * 8:(fg_ + 1) * 8, :], own=b_wA[i % 2], writes=[b_wA[i % 2]],
                      max_dma_last_dim=4096)
            loadA(0)
            for mc in range(8):
                for fg in range(4):
                    wb_ = wcnt % 2
                    wcnt += 1
                    if wcnt < len(itemsA):
                        loadA(wcnt)
                    fns = []
                    for fi in range(8):
                        fc = fg * 8 + fi
                        for ti, (c0, n) in enumerate(tiles):
                            fns.append(_mm(bank(ti)[:n, :], xT[:, fc, c0:c0 + n], wA[wb_][:, fi, :], fc == 0, fc == NCC - 1))
                    P.pe_group(fns, reads=[b_xT, b_wA[wb_]], writes=[b_ps[ti] for ti in range(5)])
                for ti, (c0, n) in enumerate(tiles):
                    cbi = ccnt % 3
                    ccnt += 1
                    rr = r0 + c0
                    P.dma("sp", hc[cbi][:n], hin_d[rr:rr + n, mc * 512:(mc + 1) * 512], own=b_hc[cbi], writes=[b_hc[cbi]])
                    P.op("dve", lambda e, ti=ti, n=n, cbi=cbi: e.tensor_tensor(out=oc[cbi][:n], in0=bank(ti)[:n, :], in1=hc[cbi][:n],
                                                                             op=ALU.add),
                         reads=[b_ps[ti], b_hc[cbi]], writes=[b_oc[cbi]])
                    bh = P.buf("h1_%d_%d" % (ti, mc))
                    b_h1[(ti, mc)] = bh
                    P.dma("sp", h1_d[rr:rr + n, mc * 512:(mc + 1) * 512], oc[cbi][:n], own=b_oc[cbi], reads=[b_oc[cbi]], writes=[bh])
            b_hrow = b_hrowA
            b_junk = b_junkA
            psb = [(i, b_ps[i]) for i in range(5, 8)]
            pscur = [0]
            for ti, (c0, n) in enumerate(tiles):
                hb = ti % 2
                rr = r0 + c0
                P.dma("sp", hrow[hb][:n], h1_d[rr:rr + n, :], own=b_hrow[hb], reads=[b_h1[(ti, mc)] for mc in range(8)],
                      writes=[b_hrow[hb]])
                emit_norm_T(P, hrow[hb], b_hrow[hb], n, st[hb], b_st[hb], junk, b_junk, gffn, identf, ps, psb,
                            xT, b_xT, c0, pscur)
            P.barrier()
            SEG = TP // 2
            def loadB(f_):
                P.dma("pool", wg[f_ % 2], wu_r[f_], own=b_wg[f_ % 2], writes=[b_wg[f_ % 2]], max_dma_last_dim=4096)
                P.dma("pool", wv[f_ % 2], wu_r[NFC + f_], own=b_wv[f_ % 2], writes=[b_wv[f_ % 2]], max_dma_last_dim=4096)
            loadB(0)
            for fc in range(NFC):
                wb_ = fc % 2
                if fc + 1 < NFC:
                    loadB(fc + 1)
                pb = 4 * (fc % 2)
                us = []
                for gi, (wt, bw) in enumerate(((wg[wb_], b_wg[wb_]), (wv[wb_], b_wv[wb_]))):
                    b0 = pb + 2 * gi
                    u = ps[:, b0 * 512 + 512 - SEG: b0 * 512 + 512 + SEG]
                    for sgi in range(2):
                        o = u[:, sgi * SEG:(sgi + 1) * SEG]
                        P.pe_group([_mm(o, wt[:, cc, :], xT[:, cc, sgi * SEG:(sgi + 1) * SEG], cc == 0, cc == NCC - 1)
                                    for cc in range(NCC)], reads=[b_xT, bw], writes=[b_ps[b0 + sgi]])
                    us.append((u, [b_ps[b0], b_ps[b0 + 1]]))
                cbase = 5 * (fc % 2)
                res = []
                for gi, (u, ub) in enumerate(us):
                    f = fc + gi * NFC
                    if ps_i == 0:
                        P.op("dve", lambda e, u=u: e.tensor_scalar(out=u[:, HALO - 2:HALO], in0=u[:, HALO - 2:HALO],
                                                                  scalar1=flag[:, 0:1], scalar2=None, op0=ALU.mult),
                             reads=ub + [bcn["flag"]], writes=ub)
                    t0, t1 = ctmp[cbase + 2 * gi], ctmp[cbase + 2 * gi + 1]
                    bt0, bt1 = b_ct[cbase + 2 * gi], b_ct[cbase + 2 * gi + 1]
                    P.op("act", lambda e, u=u, t0=t0, f=f: e.activation(out=t0[:, 2:TP], in_=u[:, 2:TP], func=AF.Identity,
                                                                       scale=cw[:, f, 2:3], bias=cb[:, f:f + 1]),
                         reads=ub + [bcn["cw"], bcn["cb"]], writes=[bt0])
                    P.op("dve", lambda e, u=u, t0=t0, t1=t1, f=f: e.scalar_tensor_tensor(out=t1[:, 2:TP], in0=u[:, 1:TP - 1],
                                                                                      scalar=cw[:, f, 1:2], in1=t0[:, 2:TP],
                                                                                      op0=ALU.mult, op1=ALU.add),
                         reads=ub + [bt0], writes=[bt1])
                    P.op("dve", lambda e, u=u, t0=t0, t1=t1, f=f: e.scalar_tensor_tensor(out=t0[:, 2:TP], in0=u[:, 0:TP - 2],
                                                                                      scalar=cw[:, f, 0:1], in1=t1[:, 2:TP],
                                                                                      op0=ALU.mult, op1=ALU.add),
                         reads=ub + [bt1], writes=[bt0])
                    res.append((t0, bt0))
                (tg, btg), (tv, btv) = res
                sgt, bsg = ctmp[cbase + 4], b_ct[cbase + 4]
                P.op("act", lambda e, tg=tg, sgt=sgt: e.activation(out=sgt[:, 2:TP], in_=tg[:, 2:TP], func=AF.Silu),
                     reads=[btg], writes=[bsg])
                P.op("pool", lambda e, fc=fc, sgt=sgt, tv=tv: e.tensor_tensor(out=aT[:, fc, 2:TP], in0=sgt[:, 2:TP], in1=tv[:, 2:TP],
                                                                            op=ALU.mult),
                     reads=[bsg, btv], writes=[b_aT])
                if ps_i == 0 and fc == 0:
                    P.op("pool", lambda e: e.memset(aT[:, :, 0:2], 0.0), writes=[b_aT])
            P.barrier()
            own_tiles = tiles[1:]
            b_h2 = {}
            wcnt = 0
            ccnt = 0
            NFG = (NFC + 7) // 8
            itemsC = [(mc_, fg_) for mc_ in range(8) for fg_ in range(NFG)]

            def loadC(i):
                mc_, fg_ = itemsC[i]
                nf_ = min(8, NFC - fg_ * 8)
                P.dma("pool", wC[i % 2][:, 0:nf_, :], wd_r[mc_, :, fg_ * 8:fg_ * 8 + nf_, :], own=b_wC[i % 2], writes=[b_wC[i % 2]],
                      max_dma_last_dim=4096)
            loadC(0)
            for mc in range(8):
                pbk = 4 * (mc % 2)
                for fg in range(NFG):
                    nf = min(8, NFC - fg * 8)
                    wb_ = wcnt % 2
                    wcnt += 1
                    if wcnt < len(itemsC):
                        loadC(wcnt)
                    fns = []
                    for fi in range(nf):
                        fc = fg * 8 + fi
                        for ti, (c0, n) in enumerate(own_tiles):
                            fns.append(_mm(bank(pbk + ti), aT[:, fc, c0:c0 + n], wC[wb_][:, fi, :], fc == 0, fc == NFC - 1))
                    P.pe_group(fns, reads=[b_aT, b_wC[wb_]], writes=[b_ps[pbk + ti] for ti in range(4)])
                for ti, (c0, n) in enumerate(own_tiles):
                    cbi = ccnt % 3
                    ccnt += 1
                    rr = r0 + c0
                    ro = rr - HALO
                    P.dma("sp", hcC[cbi], h1_d[rr:rr + n, mc * 512:(mc + 1) * 512], own=b_hcC[cbi],
                          reads=[b_h1[(ti + 1, mc)]], writes=[b_hcC[cbi]])
                    P.op("dve", lambda e, ti=ti, cbi=cbi, pbk=pbk: e.tensor_tensor(out=ocC[cbi], in0=bank(pbk + ti), in1=hcC[cbi],
                                                                                 op=ALU.add),
                         reads=[b_ps[pbk + ti], b_hcC[cbi]], writes=[b_ocC[cbi]])
                    bh = P.buf("h2_%d_%d" % (ti, mc))
                    b_h2[(ti, mc)] = bh
                    P.dma("sp", h2_d[ro:ro + n, mc * 512:(mc + 1) * 512], ocC[cbi], own=b_ocC[cbi], reads=[b_ocC[cbi]], writes=[bh])
            P.barrier()
            b_hrow = b_hrowD
            b_junk = b_junkD
            psb = [(i, b_ps[i]) for i in range(8)]
            pscur = [0]
            if last:
                P.dma("sp", gbc, gb_d, own=bcn["gbc"], writes=[bcn["gbc"]])
            for ti, (c0, n) in enumerate(own_tiles):
                hb = ti % 2
                ro = r0 + c0 - HALO
                P.dma("sp", hrow[hb], h2_d[ro:ro + n, :], own=b_hrow[hb], reads=[b_h2[(ti, mc)] for mc in range(8)],
                      writes=[b_hrow[hb]])
                if last:
                    emit_rstd(P, hrow[hb], b_hrow[hb], n, D, junk, b_junk, st[hb], b_st[hb])
                    P.op("dve", lambda e, hb=hb: e.scalar_tensor_tensor(out=hrow[hb], in0=hrow[hb], scalar=st[hb][:, 3:4], in1=gbc,
                                                                       op0=ALU.mult, op1=ALU.mult),
                         reads=[b_hrow[hb], b_st[hb], bcn["gbc"]], writes=[b_hrow[hb]])
                    P.dma("sp", out_d[ro:ro + n, :], hrow[hb], own=b_hrow[hb], reads=[b_hrow[hb]])
                else:
                    emit_norm_T(P, hrow[hb], b_hrow[hb], n, st[hb], b_st[hb], junk, b_junk, gnx, identf, ps, psb,
                                xT, b_xo, c0, pscur)
            if not last:
                P.dma("sp", xTn_r[:, :, ps_i * PASS:(ps_i + 1) * PASS], xT[:, :, HALO:TP], own=b_xo, reads=[b_xo])
            P.barrier()
        block = stack.enter_context(nc.Block())
        first = [(b.dsem, b.dcnt) for n_, b in bcn.items() if b.dsem is not None and n_ != "gbc"]
        for e in ("pe", "act", "dve", "pool"):
            P.q[e].insert(0, (first, None, None, 0))
        P.emit(block)
    return nc


_CACHE = {}


def _prog(name):
    if name not in _CACHE:
        _CACHE[name] = {"N": build_N, "M": build_M, "F0": lambda: build_F(False), "F1": lambda: build_F(True)}[name]()
    return _CACHE[name]


def _fm(v):
    return np.ascontiguousarray(np.asarray(v, np.float32).reshape(NCC, 128).T)


def _consts():
    half = HD // 2
    inv = (1.0 / (np.float32(10000.0) ** np.linspace(0.0, 1.0, half, dtype=np.float32))).astype(np.float32)
    ang = (np.arange(SEQ, dtype=np.float32)[:, None] * inv[None, :]).astype(np.float32)
    cs = np.concatenate([np.cos(ang), np.sin(ang)], axis=1).astype(np.float32)
    identf = np.eye(128, dtype=np.float32)
    identb = np.eye(128, dtype=np.float32).astype(ml_dtypes.bfloat16)
    k = np.arange(128)[:, None, None]
    r = np.arange(5)[None, :, None]
    q = np.arange(128)[None, None, :]
    kp = (r - 4) * 128 + k
    dist = q - kp
    relidx = np.clip(dist, -63, 128) + 63
    kc = np.floor_divide(kp, 64)
    qc = q // 64
    valid = (kc <= qc) & (kc >= qc - 8)
    amask = np.where(valid, 0.0, NEG).astype(np.float32).reshape(128, 640)
    return cs, identf, identb, relidx, amask


def _ret_consts(g):
    dec = np.zeros((128, 8), np.float32)
    rmask = np.zeros((128, 2, 128), np.float32)
    p = np.arange(128, dtype=np.float64)
    for hh in range(2):
        h = 2 * g + hh
        lg = np.log(np.float32(1.0) - np.float32(2.0) ** np.float32(-5.0 - h)).astype(np.float64)
        dec[:, hh] = np.exp((p + 1.0) * lg)
        dec[:, 2 + hh] = np.exp((127.0 - p) * lg) * (HD ** -0.5)
        dec[:, 4 + hh] = np.exp(128.0 * lg)
        j = p[:, None]
        i = p[None, :]
        rmask[:, hh, :] = np.where(i >= j, np.exp(-128.0 * lg), 0.0)
    return dec, rmask.reshape(128, 256)


def _run(nc, in_maps):
    res = run_bass_kernel_spmd(nc, in_maps, core_ids=list(range(NCORES)))
    return res.results


def kernel(x, ln_mix, w_in, rel_bias, w_out, ln_ffn, w_up, conv_w, conv_b, w_down, ln_final):
    x = np.asarray(x, np.float32)
    cs, identf, identb, relidx, amask = _consts()
    depth = ln_mix.shape[0]
    h = x.reshape(TOK, D)
    gfm = _fm(ln_mix[0])
    res = _run(_prog("N"), [{"rows": h[c * TOKC:(c + 1) * TOKC], "gfm": gfm, "identf": identf} for c in range(NCORES)])
    xnT = np.concatenate([res[c]["xT"] for c in range(NCORES)], axis=1)
    out = None
    for l in range(depth):
        in_maps = []
        for g in range(NCORES):
            cols = []
            for blk in range(4):
                for hh in range(2):
                    c0 = blk * 2048 + (2 * g + hh) * 128
                    cols.append(np.arange(c0, c0 + 128))
            for blk in range(3):
                for hh in range(2):
                    c0 = 8192 + blk * 2048 + (2 * g + hh) * 128
                    cols.append(np.arange(c0, c0 + 128))
            cols = np.concatenate(cols)
            wsl = np.asarray(w_in[l])[:, cols]
            wsl = np.ascontiguousarray(wsl.reshape(NCC, 128, 1792).transpose(1, 0, 2)).reshape(128, NCC * 1792)
            dec, rmask = _ret_consts(g)
            bg = np.stack([np.asarray(rel_bias[l][2 * g + hh], np.float32)[relidx] for hh in range(2)], axis=1)
            in_maps.append({"xnT": xnT, "win": wsl, "cs": cs, "dec": dec, "rmask": rmask,
                            "biasg": np.ascontiguousarray(bg.reshape(128, 2 * 640)), "amask": amask, "identb": identb})
        res = _run(_prog("M"), in_maps)
        mixT = np.empty((D, TOK), dtype=ml_dtypes.bfloat16)
        for g in range(NCORES):
            m = res[g]["mixT"]
            mixT[(2 * g) * 128:(2 * g + 2) * 128] = m[0:256]
            mixT[2048 + (2 * g) * 128:2048 + (2 * g + 2) * 128] = m[256:512]
        del res
        last = (l == depth - 1)
        wo_t = np.ascontiguousarray(np.asarray(w_out[l]).reshape(NCC, 128, 8, 512).transpose(2, 1, 0, 3)).reshape(8 * 128, NCC * 512)
        wu_t = np.ascontiguousarray(np.asarray(w_up[l]).reshape(NCC, 128, 2 * NFC, 128).transpose(2, 1, 0, 3)).reshape(2 * NFC * 128, NCC * 128)
        wd_t = np.ascontiguousarray(np.asarray(w_down[l]).reshape(NFC, 128, 8, 512).transpose(2, 1, 0, 3)).reshape(8 * 128, NFC * 512)
        cw_t = np.ascontiguousarray(np.asarray(conv_w[l], np.float32).reshape(3, 2 * NFC, 128).transpose(2, 1, 0)).reshape(128, 2 * NFC * 3)
        cb_t = np.ascontiguousarray(np.asarray(conv_b[l], np.float32).reshape(2 * NFC, 128).T)
        gffn = _fm(ln_ffn[l])
        in_maps = []
        for c in range(NCORES):
            t0 = c * TOKC
            first = (t0 % SEQ == 0)
            hin = np.zeros((TF, D), np.float32)
            mx = np.zeros((D, TF), dtype=ml_dtypes.bfloat16)
            if first:
                hin[HALO:] = h[t0:t0 + TOKC]
                mx[:, HALO:] = mixT[:, t0:t0 + TOKC]
            else:
                hin[:] = h[t0 - HALO:t0 + TOKC]
                mx[:] = mixT[:, t0 - HALO:t0 + TOKC]
            im = {"hin": hin, "mixT": mx, "wo": wo_t, "wu": wu_t, "wd": wd_t, "cw": cw_t, "cb": cb_t, "gffn": gffn,
                  "flag": np.full((128, 1), 0.0 if first else 1.0, np.float32), "identf": identf}
            if last:
                im["gbc"] = np.ascontiguousarray(np.broadcast_to(np.asarray(ln_final, np.float32)[None, :], (128, D)))
            else:
                im["gnext"] = _fm(ln_mix[l + 1])
            in_maps.append(im)
        res = _run(_prog("F1" if last else "F0"), in_maps)
        del in_maps
        if last:
            out = np.concatenate([res[c]["out"] for c in range(NCORES)], axis=0)
        else:
            h = np.concatenate([res[c]["hout"] for c in range(NCORES)], axis=0)
            xnT = np.concatenate([res[c]["xTn"] for c in range(NCORES)], axis=1)
        del res
    return out.reshape(BATCH, SEQ, D).astype(np.float32)
```
